# Optimizing a Trainium2 kernel written in Bass

```python
import jax, jax.numpy as jnp
from jax import lax
import numpy as np

D_MODEL = 2048
BATCH = 4
SEQ = 2048
DEPTH = 2

CTX_LEN = 256
GRID_W = 64
POOL_WINDOWS = (2, 4, 8, 16)
POOL_GROUP = D_MODEL // 8
POOL_WIDTH = POOL_GROUP * len(POOL_WINDOWS)
CONV_WIDTH = D_MODEL // 2
N_HEADS = 16
QK_NOPE = 128
QK_ROPE = 64
ROPE_AXIS = QK_ROPE // 2
V_DIM = 128
QK_DIM = QK_NOPE + QK_ROPE
Q_LORA = 512
KV_LORA = 512
ROPE_THETA = 10000.0
ATTN_SCALE = QK_DIM ** -0.5
Q_BLOCK = 128
D_FF = 5632
N_BRANCH = 3
EPS = 1e-6
OFF_A = 0
OFF_B = OFF_A + POOL_WIDTH
OFF_Q = OFF_B + 3 * CONV_WIDTH
OFF_KV = OFF_Q + Q_LORA
OFF_G = OFF_KV + KV_LORA + QK_ROPE
IN_COLS = OFF_G + N_BRANCH * D_MODEL

kernel_name = 'hybrid_pool_conv_mla_prefix_dit_block'


def rmsnorm(x, g):
    xf = x.astype(jnp.float32)
    y = xf * lax.rsqrt(jnp.mean(xf * xf, axis=-1, keepdims=True) + EPS)
    return (y * g.astype(jnp.float32)).astype(x.dtype)


def modulate(x, g, shift, scale):
    return rmsnorm(x, g) * (1 + scale) + shift


def conv3_centred(z, w):
    zp = jnp.pad(z, ((0, 0), (1, 1), (0, 0)))
    return zp[:, :-2] * w[0] + zp[:, 1:-1] * w[1] + zp[:, 2:] * w[2]


def multiscale_pool(a, pool_w, pool_scale):
    Bn, L, _ = a.shape
    af = a.astype(jnp.float32)
    prefix = jnp.concatenate([jnp.zeros((Bn, 1, POOL_WIDTH), jnp.float32), jnp.cumsum(af, axis=1)], axis=1)
    t = jnp.arange(L)
    outs = []
    for gi, w in enumerate(POOL_WINDOWS):
        lo = jnp.clip(t - w // 2, 0, L)
        hi = jnp.clip(t + w // 2, 0, L)
        sl = slice(gi * POOL_GROUP, (gi + 1) * POOL_GROUP)
        pg = prefix[:, :, sl]
        cnt = (hi - lo).astype(jnp.float32)[None, :, None]
        outs.append((pg[:, hi] - pg[:, lo]) / cnt - af[:, :, sl])
    pooled = jnp.stack(outs, axis=2).astype(a.dtype)
    mixed = jnp.einsum('blgc,gcd->blgd', pooled, pool_w)
    return mixed.reshape(Bn, L, POOL_WIDTH) * pool_scale


def axial_rope_tables(length):
    rows = length // GRID_W
    row = jnp.repeat(jnp.arange(rows, dtype=jnp.int32), GRID_W).astype(jnp.float32)
    col = jnp.tile(jnp.arange(GRID_W, dtype=jnp.int32), rows).astype(jnp.float32)
    inv = ROPE_THETA ** (-jnp.arange(0, ROPE_AXIS, 2, dtype=jnp.float32) / ROPE_AXIS)
    ang_r = row[:, None] * inv[None]
    ang_c = col[:, None] * inv[None]
    return (jnp.cos(ang_r), jnp.sin(ang_r), jnp.cos(ang_c), jnp.sin(ang_c))


def rotate_half(x, cos, sin):
    x1, x2 = jnp.split(x, 2, axis=-1)
    return jnp.concatenate([x1 * cos - x2 * sin, x1 * sin + x2 * cos], axis=-1)


def apply_axial_rope(x, tables):
    cr, sr, cc, sc = tables
    xf = x.astype(jnp.float32)
    out = jnp.concatenate([rotate_half(xf[..., :ROPE_AXIS], cr, sr),
                           rotate_half(xf[..., ROPE_AXIS:], cc, sc)], axis=-1)
    return out.astype(x.dtype)


def mla_q(zq, q_lora_g, w_uq, q_head_g, rope):
    cq = rmsnorm(zq, q_lora_g)
    q = rmsnorm(jnp.einsum('blr,rhd->bhld', cq, w_uq), q_head_g)
    if rope is not None:
        q = jnp.concatenate([q[..., :QK_NOPE], apply_axial_rope(q[..., QK_NOPE:], rope)], axis=-1)
    return q


def mla_kv(zkv, kv_lora_g, w_ukv, k_head_g, rope):
    ckv = rmsnorm(zkv[..., :KV_LORA], kv_lora_g)
    k_rope = zkv[..., KV_LORA:]
    kv = jnp.einsum('blr,rhd->bhld', ckv, w_ukv)
    k_nope, v = kv[..., :QK_NOPE], kv[..., QK_NOPE:]
    Bn, H, L, _ = k_nope.shape
    k = jnp.concatenate([k_nope, jnp.broadcast_to(k_rope[:, None], (Bn, H, L, QK_ROPE))], axis=-1)
    k = rmsnorm(k, k_head_g)
    if rope is not None:
        k = jnp.concatenate([k[..., :QK_NOPE], apply_axial_rope(k[..., QK_NOPE:], rope)], axis=-1)
    return k, v


def attend(q, k, v):
    s = jnp.einsum('bhqd,bhkd->bhqk', q, k).astype(jnp.float32) * ATTN_SCALE
    p = jax.nn.softmax(s, axis=-1).astype(v.dtype)
    return jnp.einsum('bhqk,bhkd->bhqd', p, v)


def blocked_attend(q, k, v):
    Bn, H, L, Dq = q.shape
    nb = L // Q_BLOCK
    qb = q.reshape(Bn, H, nb, Q_BLOCK, Dq).transpose(2, 0, 1, 3, 4)
    ob = lax.map(lambda qi: attend(qi, k, v), qb)
    return ob.transpose(1, 2, 0, 3, 4).reshape(Bn, H, L, V_DIM)


def token_mixers(h, lp, rope, ctx_kv):
    Bn, L, _ = h.shape
    z = h @ lp['w_in']
    y_a = multiscale_pool(z[..., OFF_A:OFF_B], lp['pool_w'], lp['pool_scale']) @ lp['w_branch_a']
    gate_b, gate_c, xin = jnp.split(z[..., OFF_B:OFF_Q], 3, axis=-1)
    y_b = (gate_b * conv3_centred(gate_c * xin, lp['conv_w'])) @ lp['w_branch_b']
    q = mla_q(z[..., OFF_Q:OFF_KV], lp['q_lora_g'], lp['w_uq'], lp['q_head_g'], rope)
    k, v = mla_kv(z[..., OFF_KV:OFF_G], lp['kv_lora_g'], lp['w_ukv'], lp['k_head_g'], rope)
    if ctx_kv is None:
        o = attend(q, k, v)
    else:
        k_ctx, v_ctx = ctx_kv
        o = blocked_attend(q, jnp.concatenate([k, k_ctx], axis=2), jnp.concatenate([v, v_ctx], axis=2))
    y_c = o.transpose(0, 2, 1, 3).reshape(Bn, L, N_HEADS * V_DIM) @ lp['w_branch_c']
    g = jax.nn.sigmoid(z[..., OFF_G:].astype(jnp.float32)).astype(h.dtype).reshape(Bn, L, N_BRANCH, D_MODEL)
    merged = g[..., 0, :] * y_a + g[..., 1, :] * y_b + g[..., 2, :] * y_c
    return merged @ lp['w_out'], (k, v)


def conv_ffn(h, w_up, conv, w_down):
    u, v = jnp.split(h @ w_up, 2, axis=-1)
    return (jax.nn.silu(conv3_centred(u, conv)) * v) @ w_down


def setup_inputs(seed: int = 0) -> dict:
    key = jax.random.key(seed)
    ks = jax.random.split(key, 32)
    f32 = jnp.float32

    def nrm(k, shape, scale):
        return jax.random.normal(k, shape, f32) * scale

    def gain(k, shape):
        return 1.0 + 0.05 * jax.random.normal(k, shape, f32)

    Dp = DEPTH
    return {
        'x': nrm(ks[0], (BATCH, SEQ, D_MODEL), 1.0),
        'c': nrm(ks[1], (BATCH, D_MODEL), 1.0),
        'ctx': nrm(ks[2], (BATCH, CTX_LEN, D_MODEL), 1.0),
        'c_ctx': nrm(ks[3], (D_MODEL,), 1.0),
        'norm1_g': gain(ks[4], (Dp, D_MODEL)),
        'norm2_g': gain(ks[5], (Dp, D_MODEL)),
        'w_mod': nrm(ks[6], (Dp, D_MODEL, 6 * D_MODEL), 0.5 * D_MODEL ** -0.5),
        'b_mod': nrm(ks[7], (Dp, 6 * D_MODEL), 0.01),
        'w_in': nrm(ks[8], (Dp, D_MODEL, IN_COLS), D_MODEL ** -0.5),
        'pool_w': nrm(ks[9], (Dp, len(POOL_WINDOWS), POOL_GROUP, POOL_GROUP), POOL_GROUP ** -0.5),
        'pool_scale': gain(ks[10], (Dp, POOL_WIDTH)),
        'conv_w': nrm(ks[11], (Dp, 3, CONV_WIDTH), 3 ** -0.5),
        'q_lora_g': gain(ks[12], (Dp, Q_LORA)),
        'w_uq': nrm(ks[13], (Dp, Q_LORA, N_HEADS, QK_DIM), Q_LORA ** -0.5),
        'kv_lora_g': gain(ks[14], (Dp, KV_LORA)),
        'w_ukv': nrm(ks[15], (Dp, KV_LORA, N_HEADS, QK_NOPE + V_DIM), KV_LORA ** -0.5),
        'q_head_g': gain(ks[16], (Dp, QK_DIM)),
        'k_head_g': gain(ks[17], (Dp, QK_DIM)),
        'w_branch_a': nrm(ks[18], (Dp, POOL_WIDTH, D_MODEL), POOL_WIDTH ** -0.5),
        'w_branch_b': nrm(ks[19], (Dp, CONV_WIDTH, D_MODEL), CONV_WIDTH ** -0.5),
        'w_branch_c': nrm(ks[20], (Dp, N_HEADS * V_DIM, D_MODEL), (N_HEADS * V_DIM) ** -0.5),
        'w_out': nrm(ks[21], (Dp, D_MODEL, D_MODEL), D_MODEL ** -0.5),
        'w_ffn_up': nrm(ks[22], (Dp, D_MODEL, 2 * D_FF), D_MODEL ** -0.5),
        'ffn_conv': nrm(ks[23], (Dp, 3, D_FF), 3 ** -0.5),
        'w_ffn_down': nrm(ks[24], (Dp, D_FF, D_MODEL), D_FF ** -0.5),
    }


def reference(x, c, ctx, c_ctx, norm1_g, norm2_g, w_mod, b_mod, w_in, pool_w, pool_scale, conv_w,
              q_lora_g, w_uq, kv_lora_g, w_ukv, q_head_g, k_head_g, w_branch_a, w_branch_b,
              w_branch_c, w_out, w_ffn_up, ffn_conv, w_ffn_down):
    rope = axial_rope_tables(x.shape[1])
    for i in range(DEPTH):
        last = i == DEPTH - 1
        lp = dict(w_in=w_in[i], pool_w=pool_w[i], pool_scale=pool_scale[i], conv_w=conv_w[i],
                  q_lora_g=q_lora_g[i], w_uq=w_uq[i], kv_lora_g=kv_lora_g[i], w_ukv=w_ukv[i],
                  q_head_g=q_head_g[i], k_head_g=k_head_g[i], w_branch_a=w_branch_a[i],
                  w_branch_b=w_branch_b[i], w_branch_c=w_branch_c[i], w_out=w_out[i])
        mod_x = jnp.split((jax.nn.silu(c) @ w_mod[i] + b_mod[i])[:, None, :], 6, axis=-1)
        mod_c = jnp.split(jax.nn.silu(c_ctx) @ w_mod[i] + b_mod[i], 6, axis=-1)
        sh1x, sc1x, g1x, sh2x, sc2x, g2x = mod_x
        sh1c, sc1c, g1c, sh2c, sc2c, g2c = mod_c
        hc = modulate(ctx, norm1_g[i], sh1c, sc1c)
        if last:
            k_c, v_c = mla_kv(hc @ lp['w_in'][:, OFF_KV:OFF_G], lp['kv_lora_g'], lp['w_ukv'],
                              lp['k_head_g'], None)
        else:
            ctx_mix, (k_c, v_c) = token_mixers(hc, lp, None, None)
            ctx = ctx + g1c * ctx_mix
            ctx = ctx + g2c * conv_ffn(modulate(ctx, norm2_g[i], sh2c, sc2c), w_ffn_up[i], ffn_conv[i], w_ffn_down[i])
        hx = modulate(x, norm1_g[i], sh1x, sc1x)
        x_mix, _ = token_mixers(hx, lp, rope, (k_c, v_c))
        x = x + g1x * x_mix
        x = x + g2x * conv_ffn(modulate(x, norm2_g[i], sh2x, sc2x), w_ffn_up[i], ffn_conv[i], w_ffn_down[i])
    return x
```

```python
import numpy as np
from contextlib import ExitStack
import concourse.bass as bass
import concourse.mybir as mybir
from concourse.bass_utils import run_bass_kernel_spmd

F32 = mybir.dt.float32
BF16 = mybir.dt.bfloat16
ALU = mybir.AluOpType
AF = mybir.ActivationFunctionType

SEM_EPOCH = 20000
CAST_ENG = "dve"
CUR = {"wb": None}
D = 2048
KC = 16
T = 1280
NK = 2304
HXW = 1296
DFF = 5632
NV = 306
ATTN_SCALE = 192.0 ** -0.5
EPS = 1e-6
POOL_W = (2, 4, 8, 16)
TGS = [(0, 512, 0), (512, 1024, 0), (1024, 1280, 1)]


def hxc(t):
    return 8 + t if t < 1024 else 16 + t


class Res:
    __slots__ = ("name", "lw", "rd", "dsem", "dcnt")

    ALL = []

    def __init__(self, name):
        self.name = name
        self.lw = None
        self.rd = {}
        self.dsem = None
        self.dcnt = 0
        Res.ALL.append(self)


class Prog:
    ENGS = ("pe", "act", "dve", "pool", "sp")

    def __init__(self, nc):
        self.nc = nc
        self.streams = {e: [] for e in self.ENGS}
        self.esem = {e: None for e in self.ENGS}
        self.ecnt = {e: 0 for e in self.ENGS}
        self.own = {e: set() for e in self.ENGS}
        self.waited = {e: {} for e in self.ENGS}
        self.nsem = 0

    def new_sem(self, name):
        self.nsem += 1
        return self.nc.alloc_semaphore(name=f"{name}_{self.nsem}")

    def _need(self, eng, deps):
        w = self.waited[eng]
        own = self.own[eng]
        best = {}
        for d in deps:
            tok, raw = d
            if tok is None:
                continue
            sem, val = tok
            k = id(sem)
            if k in own and (eng == "pe" or not raw):
                continue
            if w.get(k, 0) >= val:
                continue
            if k not in best or best[k][1] < val:
                best[k] = (sem, val)
        for k, (sem, val) in best.items():
            w[k] = val
            self.streams[eng].append(("wait", sem, val))

    @staticmethod
    def _deps(reads, writes):
        deps = []
        for r in reads:
            deps.append((r.lw, True))
        for r in writes:
            deps.append((r.lw, False))
            for t in r.rd.values():
                deps.append((t, False))
        return deps

    def op(self, eng, fn, reads=(), writes=()):
        self._need(eng, self._deps(reads, writes))
        if self.esem[eng] is None or self.ecnt[eng] >= SEM_EPOCH:
            self.esem[eng] = self.new_sem(eng)
            self.own[eng].add(id(self.esem[eng]))
            self.ecnt[eng] = 0
        sem = self.esem[eng]
        self.ecnt[eng] += 1
        tok = (sem, self.ecnt[eng])
        self.streams[eng].append(("op", fn, sem, 1))
        for r in reads:
            r.rd[eng] = tok
        for r in writes:
            r.lw = tok
            r.rd = {}
        return tok

    def _owned(self, q, pairs_n, reads, writes, owner, inc, items):
        deps = self._deps(reads, writes)
        if owner.dsem is None or owner.dcnt >= SEM_EPOCH:
            owner.dsem = self.new_sem("d")
            owner.dcnt = 0
        elif owner.dcnt > 0:
            deps.append(((owner.dsem, owner.dcnt), True))
        self._need(q, deps)
        sem = owner.dsem
        for it in items:
            owner.dcnt += inc
            self.streams[q].append(it + (sem,))
        tok = (sem, owner.dcnt)
        for r in reads:
            r.rd[("d", id(sem))] = tok
        for r in writes:
            r.lw = tok
            r.rd = {}
        return tok

    def dma(self, q, pairs, reads=(), writes=(), owner=None, slow=False):
        return self._owned(q, len(pairs), reads, writes, owner, 16, [("dmas" if slow else "dma", o, i) for (o, i) in pairs])

    def coll(self, fn, reads, writes, owner):
        return self._owned("pool", 1, reads, writes, owner, 1, [("coll", fn, None)])

    def wait_tok(self, eng, tok):
        self._need(eng, [(tok, True)])

    def drain_all(self, eng):
        deps = []
        for e in self.ENGS:
            if self.esem[e] is not None and e != eng:
                deps.append(((self.esem[e], self.ecnt[e]), True))
        for r in Res.ALL:
            if r.dsem is not None and r.dcnt > 0:
                deps.append(((r.dsem, r.dcnt), True))
        self._need(eng, deps)

    def finalize(self, block):
        streams = self.streams

        def play(e, name):
            for it in streams[name]:
                if it[0] == "wait":
                    e.wait_ge(it[1], it[2])
                elif it[0] == "op":
                    it[1](e).then_inc(it[2], it[3])
                elif it[0] == "dma":
                    e.dma_start(out=it[1], in_=it[2]).then_inc(it[3], 16)
                elif it[0] == "dmas":
                    e.dma_start(out=it[1], in_=it[2], allow_slow_non_contiguous=True).then_inc(it[3], 16)
                else:
                    it[1](e).then_inc(it[3], 1)

        @block.tensor
        def _(e):
            play(e, "pe")

        @block.scalar
        def _(e):
            play(e, "act")

        @block.vector
        def _(e):
            play(e, "dve")

        @block.gpsimd
        def _(e):
            play(e, "pool")

        @block.sync
        def _(e):
            play(e, "sp")


class Job:
    def __init__(self, specs, compute, extra_cast=None, bf16_src=None):
        self.specs = specs
        self.compute = compute
        self.extra_cast = extra_cast
        self.bf16_src = bf16_src


def build_program(nlayers=2, rg=None, max_jobs=None, debug=False):
    Res.ALL = []
    nc = bass.Bass("TRN2", target_bir_lowering=False)

    def din(name, shape, dt=F32):
        return nc.dram_tensor(name, shape, dt, kind="ExternalInput").ap()

    def dint(name, shape, dt=F32):
        return nc.dram_tensor(name, shape, dt, kind="ExternalOutput" if (debug and name in ("xs", "gsc", "abo", "zkvc")) else "Internal").ap()

    xT_d = din("xT", [D, 1024])
    ctxT_d = din("ctxT", [D, 256])
    cvec_d = din("cvec", [128, 80])
    oh_d = din("oh", [128, 4])
    vecs_d = din("vecs", [2, 128, NV])
    ropeq_d = din("ropeq", [2, 64, T])
    ropek_d = din("ropek", [2, 64, NK])
    rcb_d = din("rcb", [128, 128])
    mask_d = din("mask", [128, 2])
    w_mod_d = din("w_mod", [2, D, 6144])
    w_in_d = din("w_in", [2, D, 11328])
    pool_w_d = din("pool_w", [2, 1024, 256])
    w_uq_d = din("w_uq", [2, 512, 3072])
    w_ukv_d = din("w_ukv", [2, 512, 4096])
    w_ba_d = din("w_branch_a", [2, 1024, D])
    w_bb_d = din("w_branch_b", [2, 1024, D])
    w_bc_d = din("w_branch_c", [2, D, D])
    w_out_d = din("w_out", [2, D, D])
    w_up_d = din("w_ffn_up", [2, D, 2 * DFF])
    w_dn_d = din("w_ffn_down", [2, DFF, D])
    outT_d = nc.dram_tensor("outT", [D, 1024], F32, kind="ExternalOutput").ap()

    xs_d = dint("xs", [D, T])
    gsc_d = dint("gsc", [3 * D, T], BF16)
    abo_d = dint("abo", [2 * D, T], BF16)
    ex1i_d = dint("ex1i", [D, 16])
    ex1o_d = dint("ex1o", [2 * D, 16])
    ex2i_d = [dint(f"ex2i{i}", [576, 512]) for i in range(2)]
    ex2o_d = [dint(f"ex2o{i}", [1152, 512]) for i in range(2)]
    zkvc_d = dint("zkvc", [576, 256])
    wcache_d = dint("wcache", [16 * 128, 4096], BF16)
    NR = 2
    CPR = 96 // NR
    modp_d = dint("modp", [2 * CPR * 128, 8])
    modo_d = dint("modo", [NR * 2 * CPR * 128, 8])
    RG = rg if rg is not None else [[0, 1], [2, 3], [4, 5], [6, 7]]
    RG_ALL = RG

    es = ExitStack()
    with es:
        def sb(name, shape, dt):
            return es.enter_context(nc.sbuf_tensor(name, shape, dt))

        VEC = sb("VEC", [128, 2 * NV], F32)
        CV = sb("CV", [128, 80], F32)
        OH = sb("OH", [128, 4], F32)
        MP = sb("MP", [128, 16], F32)
        MODF = sb("MODF", [128, 1536], F32)
        SCB = sb("SCB", [128, 80], BF16)
        MODT = sb("MODT", [128, 384], F32)
        PAR = sb("PAR", [128, 384], F32)
        ROPEQ = sb("ROPEQ", [64, 2 * T], F32)
        RCBT = sb("RCBT", [128, 128], F32)
        MASK = sb("MASK", [128, 2], F32)
        ONESF = sb("ONESF", [128, 128], F32)
        ONESB = sb("ONESB", [128, 128], BF16)
        EPST = sb("EPST", [128, 1], F32)
        EDG = sb("EDG", [128, 16, 16], F32)
        EDG2 = sb("EDG2", [128, 16, 16], F32)
        HX = sb("HX", [128, KC * HXW], BF16)
        U1 = sb("U1", [128, 16384], BF16)
        FP = sb("FP", [128, 4608], F32)
        WS = [sb(f"WS{i}", [128, 4096], F32) for i in range(2)]
        WB = [sb(f"WB{i}", [128, 4096], BF16) for i in range(2)]
        TFt = [sb(f"TF{i}", [128, 528], F32) for i in range(8)]
        TBt = [sb(f"TB{i}", [128, 512], BF16) for i in range(4)]
        RSt = [sb(f"RS{i}", [128, 512], F32) for i in range(2)]
        GTt = [sb(f"GT{i}", [128, 3 * 512], BF16) for i in range(2)]
        H4 = sb("H4", [128, 16], F32)
        RKt = [sb(f"RK{i}", [128, 18], F32) for i in range(2)]
        SQt = [sb(f"SQ{i}", [128, 512], BF16) for i in range(2)]
        PSt = [es.enter_context(nc.psum_tensor(f"PS{i}", [128, 512], F32)) for i in range(8)]
        block = es.enter_context(nc.Block())
        P = Prog(nc)

        R = {n: Res(n) for n in ("VEC", "CV", "SCB", "MODT", "PAR", "ROPEQ", "RCBT", "MASK", "ONES", "EDG", "EDG2",
                                 "HXH", "CQ", "CKV", "KR", "SSR", "ZA", "ZB", "ZC", "ZD", "ZQ", "POOLED", "ABO",
                                 "EX1I", "EX1O", "EX2I", "EX2O", "ZKVC", "H4", "DUMMY", "MG", "FF", "OH", "MP", "MODF", "MODP", "MODO", "KRB")}
        HX_R = [Res(f"HX{i}") for i in range(3)]
        PAR_R = [Res(f"PAR{i}") for i in range(2)]
        WC_R = [Res(f"WC{i}") for i in range(16)]
        MODT_R = [Res(f"MODT{i}") for i in range(2)]
        WS_R = [Res(f"WS{i}") for i in range(2)]
        WB_R = [Res(f"WB{i}") for i in range(2)]
        TF_R = [Res(f"TF{i}") for i in range(8)]
        TB_R = [Res(f"TB{i}") for i in range(4)]
        RS_R = [Res(f"RS{i}") for i in range(2)]
        GT_R = [Res(f"GT{i}") for i in range(2)]
        SQ_R = [Res(f"SQ{i}") for i in range(2)]
        PS_R = [Res(f"PS{i}") for i in range(8)]
        XS_R = [[Res(f"XS{c}_{g}") for g in range(3)] for c in range(KC)]
        GSC_R = [[Res(f"GSC{c}_{g}") for g in range(3)] for c in range(48)]
        ABO_R = [[Res(f"ABO{c}_{g}") for g in range(3)] for c in range(32)]
        HS_R = [{n: Res(f"{n}{i}") for n in ("K", "V", "Q")} for i in range(2)]
        OUT_TOKS = []
        class _Now:
            def append(self, fn):
                fn()

            def clear(self):
                pass

            def __iter__(self):
                return iter(())
        DEFER = _Now()
        cnt = {"tf": 0, "tb": 0, "rs": 0, "gt": 0, "main": 0, "aux": 0, "prep": 0, "sq": 0}

        def sqb():
            i = cnt["sq"] % 2
            cnt["sq"] += 1
            return SQt[i], SQ_R[i]

        def tf():
            i = cnt["tf"] % 8
            cnt["tf"] += 1
            return TFt[i], TF_R[i]

        def tb():
            i = cnt["tb"] % 4
            cnt["tb"] += 1
            return TBt[i], TB_R[i]

        def rs():
            i = cnt["rs"] % 2
            cnt["rs"] += 1
            return RSt[i], RS_R[i]

        def gt():
            i = cnt["gt"] % 2
            cnt["gt"] += 1
            return GTt[i], GT_R[i]

        def psm():
            i = cnt["main"] % 6
            cnt["main"] += 1
            return PSt[i], PS_R[i]

        def psa():
            i = 6 + cnt["aux"] % 2
            cnt["aux"] += 1
            return PSt[i], PS_R[i]

        def psp():
            i = 5 + cnt["prep"] % 3
            cnt["prep"] += 1
            return PSt[i], PS_R[i]

        def mm(ps, lhsT, rhs, start, stop, reads, pres):
            if CUR["wb"] is not None:
                reads = list(reads) + [CUR["wb"]]
            P.op("pe", lambda e: e.matmul(ps, lhsT=lhsT, rhs=rhs, start=start, stop=stop), reads=reads, writes=[pres])

        def inherit(news, olds):
            def go():
                for nw in news:
                    for o in olds:
                        if o.lw is not None:
                            nw.rd[("inh", id(o), "w")] = o.lw
                        for k, t in o.rd.items():
                            nw.rd[("inh", id(o), k)] = t
            simple(go)

        def act(out, in_, func, reads, writes, bias=None, scale=None):
            kw = {}
            if bias is not None:
                kw["bias"] = bias
            if scale is not None:
                kw["scale"] = scale
            P.op("act", lambda e: e.activation(out=out, in_=in_, func=func, **kw), reads=reads, writes=writes)

        def tt(out, in0, in1, op, reads, writes, eng="dve"):
            P.op(eng, lambda e: e.tensor_tensor(out=out, in0=in0, in1=in1, op=op), reads=reads, writes=writes)

        def ts(out, in0, s1, s2, op0, op1, reads, writes, eng="dve"):
            if s2 is None:
                P.op(eng, lambda e: e.tensor_scalar(out=out, in0=in0, scalar1=s1, scalar2=None, op0=op0), reads=reads, writes=writes)
            else:
                P.op(eng, lambda e: e.tensor_scalar(out=out, in0=in0, scalar1=s1, scalar2=s2, op0=op0, op1=op1), reads=reads, writes=writes)

        def stt(out, in0, scalar, in1, op0, op1, reads, writes, eng="dve"):
            P.op(eng, lambda e: e.scalar_tensor_tensor(out=out, in0=in0, scalar=scalar, in1=in1, op0=op0, op1=op1),
                 reads=reads, writes=writes)

        def cp(out, in_, reads, writes, eng="dve"):
            P.op(eng, lambda e: e.tensor_copy(out=out, in_=in_), reads=reads, writes=writes)

        def recip(out, in_, reads, writes):
            P.op("dve", lambda e: e.reciprocal(out=out, in_=in_), reads=reads, writes=writes)

        def memset(ap, val, writes, eng="pool"):
            P.op(eng, lambda e: e.memset(ap, val), writes=writes)

        def rstd_from(ps_ap, n, dim, pres, parts=128):
            rt, rr = rs()
            act(rt[0:parts, 0:n], ps_ap, AF.Ln, [pres, R["ONES"]], [rr], bias=EPST[0:parts, 0:1], scale=1.0 / dim)
            act(rt[0:parts, 0:n], rt[0:parts, 0:n], AF.Exp, [rr], [rr], scale=-0.5)
            return rt, rr

        def vcol(l, c, parts=128):
            return VEC[0:parts, l * NV + c: l * NV + c + 1]

        def xs_view(k, t0, t1):
            return xs_d[k * 128:(k + 1) * 128, t0:t1]

        def wview(w2d, r0, kc, c0, ncols):
            return w2d[r0:r0 + kc * 128, c0:c0 + ncols].rearrange("(k p) c -> p k c", p=128)

        JOBS = []
        SINK = [JOBS]

        class _JP:
            def append(self, j):
                SINK[-1].append(j)
        JOBSP = _JP()

        def simple(fn):
            SINK[-1].append(Job(None, lambda wb: fn()))

        def setup():
            for l in range(2):
                P.dma("sp", [(VEC[:, l * NV:(l + 1) * NV], vecs_d[l])], writes=[R["VEC"]], owner=R["VEC"])
            P.dma("sp", [(CV[:], cvec_d[:, :])], writes=[R["CV"]], owner=R["CV"])
            P.dma("sp", [(ROPEQ[:, 0:T], ropeq_d[0]), (ROPEQ[:, T:2 * T], ropeq_d[1])], writes=[R["ROPEQ"]], owner=R["ROPEQ"])
            P.dma("sp", [(RCBT[:], rcb_d[:, :])], writes=[R["RCBT"]], owner=R["RCBT"])
            P.dma("sp", [(MASK[:], mask_d[:, :])], writes=[R["MASK"]], owner=R["MASK"])
            memset(ONESF[:], 1.0, [R["ONES"]])
            memset(ONESB[:], 1.0, [R["ONES"]])
            memset(EPST[:], EPS, [R["ONES"]])
            memset(HX[:], 0.0, HX_R + [R["HXH"]])
            for j in range(5):
                act(SCB[:, j:80:5], CV[:, j * 16:(j + 1) * 16], AF.Silu, [R["CV"]], [R["SCB"]])
            P.dma("sp", [(OH[:], oh_d[:, :])], writes=[R["OH"]], owner=R["OH"])
            memset(MP[:], 0.0, [R["MP"]])
            allx = [XS_R[c][g] for c in range(KC) for g in range(3)]
            P.dma("sp", [(xs_d[:, 0:1024], xT_d[:, :]), (xs_d[:, 1024:1280], ctxT_d[:, :])], writes=allx, owner=R["DUMMY"])

        simple(setup)

        def exchange_edges(ncol):
            def go():
                allx0 = [XS_R[c][0] for c in range(KC)]
                allx1 = [XS_R[c][1] for c in range(KC)]
                xv = xs_d.rearrange("(k p) t -> p k t", p=128)
                P.dma("sp", [(EDG[:, :, 0:ncol], xv[:, :, 0:ncol]), (EDG[:, :, 8:8 + ncol], xv[:, :, 1024 - ncol:1024])],
                      reads=allx0 + allx1, writes=[R["EDG"]], owner=R["EDG"], slow=True)
                ev = ex1i_d.rearrange("(k p) t -> p k t", p=128)
                P.dma("pool", [(ev, EDG[:])], reads=[R["EDG"]], writes=[R["EX1I"]], owner=R["EDG"])
                P.coll(lambda e: e.collective_compute("AllGather", ALU.bypass, replica_groups=RG, ins=[ex1i_d], outs=[ex1o_d]),
                       reads=[R["EX1I"]], writes=[R["EX1O"]], owner=R["EX1O"])
                ov = ex1o_d.rearrange("(r k p) t -> r p k t", p=128, r=2)
                P.dma("sp", [(EDG2[:, :, 8 - ncol:8], ov[0][:, :, 8:8 + ncol]), (EDG2[:, :, 8:8 + ncol], ov[1][:, :, 0:ncol])],
                      reads=[R["EX1O"]], writes=[R["EDG2"]], owner=R["EDG2"], slow=True)
            simple(go)

        def norm_phase(l, which, tgs, nh):
            qa, qb = (0, 1) if which == 0 else (3, 4)

            def par(q, seg, k):
                c = l * 192 + q * 32 + seg * 16 + k
                return PAR[:, c:c + 1]

            def tg_stages(t0, t1, seg, gi):
                n = t1 - t0
                c0 = hxc(t0)
                st = {}

                def p1():
                    st["pt"], st["pr"] = psm()
                    pt, pr = st["pt"], st["pr"]
                    for k in range(KC):
                        xk, xr = tf()
                        P.dma("sp", [(xk[:, 0:n], xs_view(k, t0, t1))], reads=[XS_R[k][gi]], writes=[xr], owner=xr)
                        sq, sr = tf()
                        if k % 2 == 0:
                            act(sq[:, 0:n], xk[:, 0:n], AF.Square, [xr], [sr])
                        else:
                            tt(sq[:, 0:n], xk[:, 0:n], xk[:, 0:n], ALU.mult, [xr], [sr])
                        mm(pt[:, 0:n], ONESF[:], sq[:, 0:n], k == 0, k == KC - 1, [sr, R["ONES"]], pr)

                def rst():
                    st["rt"], st["rr"] = rstd_from(st["pt"][:, 0:n], n, D, st["pr"])

                def p2():
                    rt, rr = st["rt"], st["rr"]
                    for k in range(KC):
                        xk, xr = tf()
                        P.dma("sp", [(xk[:, 0:n], xs_view(k, t0, t1))], reads=[XS_R[k][gi]], writes=[xr], owner=xr)
                        tm, tr = tf()
                        tt(tm[:, 0:n], xk[:, 0:n], rt[:, 0:n], ALU.mult, [xr, rr], [tr])
                        act(HX[:, k * HXW + c0:k * HXW + c0 + n], tm[:, 0:n], AF.Identity, [tr, PAR_R[l]], [HX_R[gi]],
                            bias=par(qb, seg, k), scale=par(qa, seg, k))
                return p1, rst, p2

            def halo_stages():
                lo, hi = 8 - nh, 8 + nh
                n = 2 * nh
                st = {}

                def p1():
                    st["pt"], st["pr"] = psm()
                    pt, pr = st["pt"], st["pr"]
                    for k in range(KC):
                        sq, sr = tf()
                        act(sq[:, 0:n], EDG2[:, k, lo:hi], AF.Square, [R["EDG2"]], [sr])
                        mm(pt[:, 0:n], ONESF[:], sq[:, 0:n], k == 0, k == KC - 1, [sr, R["ONES"]], pr)

                def rst():
                    st["rt"], st["rr"] = rstd_from(st["pt"][:, 0:n], n, D, st["pr"])

                def p2():
                    rt, rr = st["rt"], st["rr"]
                    for k in range(KC):
                        tm, tr = tf()
                        tt(tm[:, 0:n], EDG2[:, k, lo:hi], rt[:, 0:n], ALU.mult, [R["EDG2"], rr], [tr])
                        t2, t2r = tf()
                        act(t2[:, 0:n], tm[:, 0:n], AF.Identity, [tr, PAR_R[l]], [t2r], bias=par(qb, 0, k), scale=par(qa, 0, k))
                        ts(HX[:, k * HXW + lo:k * HXW + 8], t2[:, 0:nh], MASK[:, 0:1], None, ALU.mult, None,
                           [t2r, R["MASK"]], [R["HXH"]])
                        ts(HX[:, k * HXW + 1032:k * HXW + 1032 + nh], t2[:, nh:n], MASK[:, 1:2], None, ALU.mult, None,
                           [t2r, R["MASK"]], [R["HXH"]])
                return p1, rst, p2

            def go():
                items = [tg_stages(t0, t1, seg, gi) for gi, (t0, t1, seg) in enumerate(TGS) if gi in tgs]
                items.append(halo_stages())
                for w0 in range(0, len(items), 2):
                    wave = items[w0:w0 + 2]
                    for it_ in wave:
                        it_[0]()
                    for it_ in wave:
                        it_[1]()
                    for it_ in wave:
                        it_[2]()
            simple(go)

        def mods_blocks(l, b0, b1):
            for b in range(b0, b1):
                def comp(wb, l=l, b=b):
                    pt, pr = psa()
                    for m in range(2):
                        for k in range(KC):
                            mm(pt[:, 8 * m:8 * m + 5], wb[:, k * 256 + m * 128:k * 256 + m * 128 + 128],
                               SCB[:, 5 * k:5 * k + 5], k == 0, k == KC - 1, [R["SCB"]], pr)
                    for m in range(2):
                        act(MP[:, 8 * m:8 * m + 5], pt[:, 8 * m:8 * m + 5], AF.Copy, [pr], [R["MP"]])
                    r0 = l * CPR * 128 + 2 * b * 128
                    P.dma("pool", [(modp_d[r0:r0 + 256, :].rearrange("(m p) j -> p m j", p=128),
                                    MP[:].rearrange("p (m j) -> p m j", m=2))],
                          reads=[R["MP"]], writes=[R["MODP"]], owner=R["MP"], slow=True)
                JOBSP.append(Job([(wview(w_mod_d[l], 0, KC, b * 256, 256), 0, KC, 256)], comp))

        def mods_gather(l, whs):
            def gather():
                P.coll(lambda e: e.collective_compute("AllGather", ALU.bypass, replica_groups=RG_ALL, ins=[modp_d], outs=[modo_d]),
                       reads=[R["MODP"]], writes=[R["MODO"]], owner=R["MODO"])
                ov = modo_d.rearrange("(r l c p) j -> r l p c j", r=NR, l=2, c=CPR, p=128)
                pairs = []
                for r in range(NR):
                    for wh in whs:
                        o0 = l * 768 + (wh * 16 + r * 8) * 8
                        pairs.append((MODF[:, o0:o0 + 64].rearrange("p (c j) -> p c j", c=8), ov[r][l][:, wh * 8:(wh + 1) * 8, :]))
                P.dma("sp", pairs, reads=[R["MODO"]], writes=[R["MODF"]], owner=R["MODF"], slow=True)
                mo = l * 192
                mf = l * 768
                for wh in whs:
                    a0, a1 = mo + 32 * wh, mo + 32 * wh + 32
                    f0, f1 = mf + 128 * wh, mf + 128 * wh + 128
                    cp(MODT[:, a0 + 1:a1:2], MODF[:, f0 + 4:f1:8], [R["MODF"]], [MODT_R[l]])
                    ts(MODT[:, a0:a1:2], MODF[:, f0:f1:8], OH[:, 0:1], None, ALU.mult, None, [R["MODF"], R["OH"]], [MODT_R[l]])
                    for j in range(1, 4):
                        stt(MODT[:, a0:a1:2], MODF[:, f0 + j:f1:8], OH[:, j:j + 1], MODT[:, a0:a1:2],
                            ALU.mult, ALU.add, [R["MODF"], R["OH"], MODT_R[l]], [MODT_R[l]])
            simple(gather)

        def mods_final(l, part):
            mo = l * 192
            po = l * 192

            def go():
                c0, c1 = (0, 64) if part == 0 else (64, 192)
                for w in range(2):
                    tt(MODT[:, mo + c0 + w:mo + c1:2], MODT[:, mo + c0 + w:mo + c1:2],
                       VEC[:, l * NV + 32 + c0 // 2:l * NV + 32 + c1 // 2], ALU.add, [MODT_R[l], R["VEC"]], [MODT_R[l]])
                for w in range(2):
                    for (q, wsc, gcol) in (((0, 1, 0),) if part == 0 else ((3, 4, 16),)):
                        stt(PAR[:, po + q * 32 + w * 16:po + q * 32 + w * 16 + 16], MODT[:, mo + 32 * wsc + w:mo + 32 * wsc + 32:2], 1.0,
                            VEC[:, l * NV + gcol:l * NV + gcol + 16], ALU.add, ALU.mult, [MODT_R[l], R["VEC"]], [PAR_R[l]])
                    for (q, wh) in (((1, 0),) if part == 0 else ((2, 2), (4, 3), (5, 5))):
                        cp(PAR[:, po + q * 32 + w * 16:po + q * 32 + w * 16 + 16], MODT[:, mo + 32 * wh + w:mo + 32 * wh + 32:2],
                           [MODT_R[l]], [PAR_R[l]])
            simple(go)

        def hx_mm(ps, wb, off, kc_cols, m_off, mcols, c0, n, pres, gi_reads):
            for k in range(KC):
                mm(ps, wb[:, off + k * kc_cols + m_off:off + k * kc_cols + m_off + mcols],
                   HX[:, k * HXW + c0:k * HXW + c0 + n], k == 0, k == KC - 1, gi_reads, pres)

        def b1_phase(l):
            wi = w_in_d[l]
            for b, (c0w, ncols) in enumerate(((4608, 256), (4864, 256), (5120, 64))):
                def comp(wb, b=b, ncols=ncols):
                    for m in range((ncols + 127) // 128):
                        rows = min(128, ncols - m * 128)
                        ch = 2 * b + m
                        for gi, (t0, t1, seg) in enumerate(TGS):
                            n = t1 - t0
                            pt, pr = psm()
                            hx_mm(pt[0:rows, 0:n], wb, 0, ncols, m * 128, rows, hxc(t0), n, pr, [HX_R[gi]])
                            t_, tr = tf()
                            act(t_[0:rows, 0:n], pt[0:rows, 0:n], AF.Copy, [pr], [tr])
                            if seg == 0:
                                dst, dr = ex2i_d[gi][ch * 128:ch * 128 + rows, 0:512], R["EX2I"]
                            else:
                                dst, dr = zkvc_d[ch * 128:ch * 128 + rows, 0:256], R["ZKVC"]
                            DEFER.append(lambda t_=t_, tr=tr, dst=dst, dr=dr, rows=rows, n=n:
                                         P.dma("pool", [(dst, t_[0:rows, 0:n])], reads=[tr], writes=[], owner=tr))
                            EXW.append(tr)
                JOBSP.append(Job([(wview(wi, 0, KC, c0w, ncols), 0, KC, ncols)], comp))

            def coll():
                deps = []
                for tr in EXW:
                    deps.append(((tr.dsem, tr.dcnt), True))
                P._need("pool", deps)
                EXW.clear()
                for i in range(2):
                    P.coll(lambda e, i=i: e.collective_compute("AllGather", ALU.bypass, replica_groups=RG,
                                                               ins=[ex2i_d[i]], outs=[ex2o_d[i]]),
                           reads=[R["EX2I"]], writes=[R["EX2O"], R["ZKVC"]], owner=R["EX2O"])
            simple(coll)

        EXW = []

        def pool_chunk(Z, zr, n, wi_, seg, dst_off):
            w = POOL_W[wi_]
            ZB, ZC = FP[:, 1040:2080], FP[:, 2080:3120]
            tt(ZB[:, 1:n + 16], Z[:, 0:n + 15], Z[:, 1:n + 16], ALU.add, [zr], [R["ZB"]])
            S, Sr = ZB, R["ZB"]
            if w >= 4:
                tt(ZC[:, 2:n + 15], ZB[:, 1:n + 14], ZB[:, 3:n + 16], ALU.add, [R["ZB"]], [R["ZC"]])
                S, Sr = ZC, R["ZC"]
            if w >= 8:
                tt(ZB[:, 4:n + 13], ZC[:, 2:n + 11], ZC[:, 6:n + 15], ALU.add, [R["ZC"]], [R["ZB"]])
                S, Sr = ZB, R["ZB"]
            if w >= 16:
                tt(ZC[:, 8:n + 8], ZB[:, 4:n + 4], ZB[:, 12:n + 12], ALU.add, [R["ZB"]], [R["ZC"]])
                S, Sr = ZC, R["ZC"]
            stt(U1[:, dst_off:dst_off + n], S[:, 8:n + 8], 1.0 / w, Z[:, 8:n + 8], ALU.mult, ALU.subtract,
                [Sr, zr], [R["POOLED"]])
            for side in range(2):
                a = 0 if side == 0 else n - 8
                rc = RCBT[:, (seg * 4 + wi_) * 16 + side * 8:(seg * 4 + wi_) * 16 + side * 8 + 8]
                t_, tr = tf()
                tt(t_[:, 0:8], S[:, 8 + a:16 + a], rc, ALU.mult, [Sr, R["RCBT"]], [tr])
                tt(U1[:, dst_off + a:dst_off + a + 8], t_[:, 0:8], Z[:, 8 + a:16 + a], ALU.subtract, [tr, zr], [R["POOLED"]])

        def pool_phase(l, tgs):
            wi = w_in_d[l]
            ZA, ZD = FP[:, 0:1040], FP[:, 3120:4160]
            for b in range(4):
                def comp(wb, b=b):
                    for m in range(2):
                        j = 2 * b + m
                        pts = []
                        for gi in (0, 1):
                            t0, t1, seg = TGS[gi]
                            pt, pr = psm()
                            hx_mm(pt[:, 0:512], wb, 0, 256, m * 128, 128, hxc(t0), 512, pr, [HX_R[gi]])
                            pts.append((pt, pr))
                        ph, phr = psa()
                        for side, c0 in enumerate((0, 1032)):
                            hx_mm(ph[:, side * 8:side * 8 + 8], wb, 0, 256, m * 128, 128, c0, 8, phr, [R["HXH"]])
                        act(ZA[:, 8:520], pts[0][0][:, 0:512], AF.Copy, [pts[0][1]], [R["ZA"]])
                        cp(ZA[:, 520:1032], pts[1][0][:, 0:512], [pts[1][1]], [R["ZA"]])
                        act(ZA[:, 0:8], ph[:, 0:8], AF.Copy, [phr], [R["ZA"]])
                        act(ZA[:, 1032:1040], ph[:, 8:16], AF.Copy, [phr], [R["ZA"]])
                        pool_chunk(ZA, R["ZA"], 1024, j // 2, 0, j * T)
                        if 2 in tgs:
                            pt, pr = psm()
                            hx_mm(pt[:, 0:256], wb, 0, 256, m * 128, 128, hxc(1024), 256, pr, [HX_R[2]])
                            memset(ZD[:, 0:8], 0.0, [R["ZD"]], eng="dve")
                            memset(ZD[:, 264:272], 0.0, [R["ZD"]], eng="dve")
                            act(ZD[:, 8:264], pt[:, 0:256], AF.Copy, [pr], [R["ZD"]])
                            pool_chunk(ZD, R["ZD"], 256, j // 2, 1, j * T + 1024)
                JOBSP.append(Job([(wview(wi, 0, KC, b * 256, 256), 0, KC, 256)], comp))
                if b % 1 == 0:
                    g = b
                    def mix(wb, g=g):
                        for mo in range(2):
                            for gi, (t0, t1, seg) in enumerate(TGS):
                                if gi not in tgs:
                                    continue
                                n = t1 - t0
                                pt, pr = psm()
                                for ki in range(2):
                                    mm(pt[:, 0:n], wb[:, ki * 256 + mo * 128:ki * 256 + mo * 128 + 128],
                                       U1[:, (2 * g + ki) * T + t0:(2 * g + ki) * T + t1], ki == 0, ki == 1, [R["POOLED"]], pr)
                                o_, orr = tb()
                                act(o_[:, 0:n], pt[:, 0:n], AF.Identity, [pr, R["VEC"]], [orr], scale=vcol(l, 128 + 2 * g + mo))
                                ch = 2 * g + mo
                                DEFER.append(lambda o_=o_, orr=orr, ch=ch, t0=t0, t1=t1, gi=gi, n=n:
                                             P.dma("pool", [(abo_d[ch * 128:(ch + 1) * 128, t0:t1], o_[:, 0:n])],
                                                   reads=[orr], writes=[ABO_R[ch][gi]], owner=orr))
                    JOBSP.append(Job([(wview(pool_w_d[l], g * 256, 2, 0, 256), 0, 2, 256)], mix))

        def conv_phase(l, tgs):
            wi = w_in_d[l]
            UU, CVL, UC, CVC = FP[:, 0:1040], FP[:, 3120:4160], FP[:, 1040:1300], FP[:, 2080:2340]

            def conv3(dst, src, n, j, sr, dr):
                ts(dst[:, 0:n], src[:, 1:n + 1], vcol(l, 136 + 8 + j), None, ALU.mult, None, [sr, R["VEC"]], [dr])
                stt(dst[:, 0:n], src[:, 0:n], vcol(l, 136 + j), dst[:, 0:n], ALU.mult, ALU.add, [sr, R["VEC"], dr], [dr])
                stt(dst[:, 0:n], src[:, 2:n + 2], vcol(l, 136 + 16 + j), dst[:, 0:n], ALU.mult, ALU.add, [sr, R["VEC"], dr], [dr])

            for j in range(8):
                def comp1(wb, j=j):
                    ph, phr = psa()
                    for q in range(2):
                        for side, c0 in enumerate((7, 1032)):
                            hx_mm(ph[:, 2 * q + side:2 * q + side + 1], wb, q * 2048, 128, 0, 128, c0, 1, phr, [R["HXH"]])
                    act(H4[:, 0:4], ph[:, 0:4], AF.Copy, [phr], [R["H4"]])
                    tt(UU[:, 0:1], H4[:, 0:1], H4[:, 2:3], ALU.mult, [R["H4"]], [R["ZA"]])
                    tt(UU[:, 1025:1026], H4[:, 1:2], H4[:, 3:4], ALU.mult, [R["H4"]], [R["ZA"]])
                    for gi in (0, 1):
                        t0, t1, seg = TGS[gi]
                        pa, par_ = psm()
                        hx_mm(pa[:, 0:512], wb, 0, 128, 0, 128, hxc(t0), 512, par_, [HX_R[gi]])
                        pb, pbr = psm()
                        hx_mm(pb[:, 0:512], wb, 2048, 128, 0, 128, hxc(t0), 512, pbr, [HX_R[gi]])
                        t_, tr = tf()
                        act(t_[:, 0:512], pa[:, 0:512], AF.Copy, [par_], [tr])
                        tt(UU[:, 1 + t0:1 + t1], t_[:, 0:512], pb[:, 0:512], ALU.mult, [tr, pbr], [R["ZA"]])
                    conv3(CVL, UU, 1024, j, R["ZA"], R["ZD"])
                    if 2 in tgs:
                        pa, par_ = psm()
                        hx_mm(pa[:, 0:256], wb, 0, 128, 0, 128, hxc(1024), 256, par_, [HX_R[2]])
                        pb, pbr = psm()
                        hx_mm(pb[:, 0:256], wb, 2048, 128, 0, 128, hxc(1024), 256, pbr, [HX_R[2]])
                        t_, tr = tf()
                        act(t_[:, 0:256], pa[:, 0:256], AF.Copy, [par_], [tr])
                        memset(UC[:, 0:1], 0.0, [R["ZB"]], eng="dve")
                        memset(UC[:, 257:258], 0.0, [R["ZB"]], eng="dve")
                        tt(UC[:, 1:257], t_[:, 0:256], pb[:, 0:256], ALU.mult, [tr, pbr], [R["ZB"]])
                        conv3(CVC, UC, 256, j, R["ZB"], R["ZC"])
                JOBSP.append(Job([(wview(wi, 0, KC, 2048 + j * 128, 128), 0, KC, 128),
                                 (wview(wi, 0, KC, 3072 + j * 128, 128), 2048, KC, 128)], comp1))

                def comp2(wb, j=j):
                    for gi, (t0, t1, seg) in enumerate(TGS):
                        if gi not in tgs:
                            continue
                        n = t1 - t0
                        pt, pr = psm()
                        hx_mm(pt[:, 0:n], wb, 0, 128, 0, 128, hxc(t0), n, pr, [HX_R[gi]])
                        o_, orr = tb()
                        if seg == 0:
                            tt(o_[:, 0:n], pt[:, 0:n], CVL[:, t0:t1], ALU.mult, [pr, R["ZD"]], [orr])
                        else:
                            tt(o_[:, 0:n], pt[:, 0:n], CVC[:, 0:256], ALU.mult, [pr, R["ZC"]], [orr])
                        ch = 8 + j
                        DEFER.append(lambda o_=o_, orr=orr, ch=ch, t0=t0, t1=t1, gi=gi, n=n:
                                     P.dma("pool", [(abo_d[ch * 128:(ch + 1) * 128, t0:t1], o_[:, 0:n])],
                                           reads=[orr], writes=[ABO_R[ch][gi]], owner=orr))
                JOBSP.append(Job([(wview(wi, 0, KC, 1024 + j * 128, 128), 0, KC, 128)], comp2))

        def gates_phase(l, tgs):
            wi = w_in_d[l]
            for b in range(24):
                def comp(wb, b=b):
                    for m in range(2):
                        ch = 2 * b + m
                        for gi, (t0, t1, seg) in enumerate(TGS):
                            if gi not in tgs:
                                continue
                            n = t1 - t0
                            pt, pr = psm()
                            hx_mm(pt[:, 0:n], wb, 0, 256, m * 128, 128, hxc(t0), n, pr, [HX_R[gi]])
                            o_, orr = tb()
                            act(o_[:, 0:n], pt[:, 0:n], AF.Sigmoid, [pr], [orr])
                            DEFER.append(lambda o_=o_, orr=orr, ch=ch, t0=t0, t1=t1, gi=gi, n=n:
                                         P.dma("pool", [(gsc_d[ch * 128:(ch + 1) * 128, t0:t1], o_[:, 0:n])],
                                               reads=[orr], writes=[GSC_R[ch][gi]], owner=orr))
                JOBSP.append(Job([(wview(wi, 0, KC, 5184 + b * 256, 256), 0, KC, 256)], comp))

        def q_phase(l, tgs):
            wi = w_in_d[l]
            for gi, (t0, t1, seg) in enumerate(TGS):
                if gi not in tgs:
                    continue
                for b in range(2):
                    def comp(wb, b=b, gi=gi, t0=t0, t1=t1):
                        n = t1 - t0
                        for m in range(2):
                            c = 2 * b + m
                            pt, pr = psm()
                            hx_mm(pt[:, 0:n], wb, 0, 256, m * 128, 128, hxc(t0), n, pr, [HX_R[gi]])
                            act(FP[:, c * 512:c * 512 + n], pt[:, 0:n], AF.Copy, [pr], [R["ZA"], R["ZB"]])
                        if b == 1:
                            pt, pr = psm()
                            for c in range(4):
                                sq, sr = tf()
                                act(sq[:, 0:n], FP[:, c * 512:c * 512 + n], AF.Square, [R["ZA"], R["ZB"]], [sr])
                                mm(pt[:, 0:n], ONESF[:], sq[:, 0:n], c == 0, c == 3, [sr, R["ONES"]], pr)
                            rt, rr = rstd_from(pt[:, 0:n], n, 512, pr)
                            for c in range(4):
                                tm, tr = tf()
                                tt(tm[:, 0:n], FP[:, c * 512:c * 512 + n], rt[:, 0:n], ALU.mult, [R["ZA"], R["ZB"], rr], [tr])
                                act(U1[:, c * T + t0:c * T + t1], tm[:, 0:n], AF.Identity, [tr, R["VEC"]], [R["CQ"]],
                                    scale=vcol(l, 160 + c))
                    JOBSP.append(Job([(wview(wi, 0, KC, 4096 + b * 256, 256), 0, KC, 256)], comp))

        CKV0 = 5120
        KRB0 = 14336
        SQKR0 = 16640

        def kv_prep(l):
            def go():
                for kg in range(5):
                    n = 512 if kg < 4 else 256
                    k0 = kg * 512
                    if kg < 4:
                        r = kg // 2
                        src = ex2o_d[kg % 2][r * 576:(r + 1) * 576, :]
                        sres = R["EX2O"]
                    else:
                        src = zkvc_d[:, :]
                        sres = R["ZKVC"]
                    pt, pr = psm()
                    for c in range(4):
                        zk, zr = tf()
                        P.dma("sp", [(zk[:, 0:n], src[c * 128:(c + 1) * 128, :])], reads=[sres], writes=[zr], owner=zr)
                        sq, sr = tf()
                        act(sq[:, 0:n], zk[:, 0:n], AF.Square, [zr], [sr])
                        mm(pt[:, 0:n], ONESF[:], sq[:, 0:n], c == 0, c == 3, [sr, R["ONES"]], pr)
                    rt, rr = rstd_from(pt[:, 0:n], n, 512, pr)
                    for c in range(4):
                        zk, zr = tf()
                        P.dma("sp", [(zk[:, 0:n], src[c * 128:(c + 1) * 128, :])], reads=[sres], writes=[zr], owner=zr)
                        tm, tr = tf()
                        tt(tm[:, 0:n], zk[:, 0:n], rt[:, 0:n], ALU.mult, [zr, rr], [tr])
                        act(U1[:, CKV0 + c * NK + k0:CKV0 + c * NK + k0 + n], tm[:, 0:n], AF.Identity, [tr, R["VEC"]],
                            [R["CKV"]], scale=vcol(l, 164 + c))
                    kr, krr = tf()
                    P.dma("sp", [(kr[0:64, 0:n], src[512:576, :])], reads=[sres], writes=[krr], owner=krr)
                    act(HX[0:64, SQKR0 + k0:SQKR0 + k0 + n], kr[0:64, 0:n], AF.Square, [krr], [R["KRB"]])
                    co, cor = tf()
                    P.dma("sp", [(co[0:64, 0:n], ropek_d[0][:, k0:k0 + n])], writes=[cor], owner=cor)
                    t1_, t1r = tf()
                    stt(t1_[0:64, 0:n], kr[0:64, 0:n], vcol(l, 172, 64), co[0:64, 0:n], ALU.mult, ALU.mult,
                        [krr, cor, R["VEC"]], [t1r])
                    ks, ksr = tf()
                    P.dma("sp", [(ks[0:16, 0:n], src[528:544, :]), (ks[16:32, 0:n], src[512:528, :]),
                                 (ks[32:48, 0:n], src[560:576, :]), (ks[48:64, 0:n], src[544:560, :])],
                          reads=[sres], writes=[ksr], owner=ksr)
                    si, sir = tf()
                    P.dma("sp", [(si[0:64, 0:n], ropek_d[1][:, k0:k0 + n])], writes=[sir], owner=sir)
                    t2_, t2r = tf()
                    stt(t2_[0:64, 0:n], ks[0:64, 0:n], vcol(l, 173, 64), si[0:64, 0:n], ALU.mult, ALU.mult,
                        [ksr, sir, R["VEC"]], [t2r])
                    tt(HX[0:64, KRB0 + k0:KRB0 + k0 + n], t1_[0:64, 0:n], t2_[0:64, 0:n], ALU.add, [t1r, t2r], [R["KRB"]])
            simple(go)

        def attn_phase(l, tgs):
            def tiles(h):
                hs = h % 2
                base = hs * 7168
                return dict(KTN=HX[:, base:base + NK], KTR=HX[0:64, KRB0:KRB0 + NK], RK=RKt[hs],
                            VV=HX[:, base + 2304:base + 2304 + NK], QTN=HX[:, base + 4608:base + 4608 + T],
                            QTR=HX[0:64, base + 5888:base + 5888 + T], hr=HS_R[hs])

            def xcast(wsb, wbb):
                src = wsb[:, 1024:1792].rearrange("p (k c) -> p k c", k=4)
                dst = wbb[:, 1792:2048].rearrange("p (k c) -> p k c", k=4)
                for (d0, s0) in ((0, 144), (16, 128), (32, 176), (48, 160)):
                    yield dst[:, :, d0:d0 + 16], src[:, :, s0:s0 + 16]

            def prep_steps(h, wb):
                tl = tiles(h)
                KTN, KTR, VV, QTN, QTR, hr = tl["KTN"], tl["KTR"], tl["VV"], tl["QTN"], tl["QTR"], tl["hr"]
                RK = tl["RK"]
                rkp, rkr = PSt[4], PS_R[4]
                for kg in range(5):
                    n = 512 if kg < 4 else 256
                    k0 = kg * 512
                    pt, pr = psp()
                    for k in range(4):
                        mm(pt[:, 0:n], wb[:, k * 256:k * 256 + 128], U1[:, CKV0 + k * NK + k0:CKV0 + k * NK + k0 + n],
                           k == 0, k == 3, [R["CKV"]], pr)
                    yield
                    sq, sr = sqb()
                    act(sq[:, 0:n], pt[:, 0:n], AF.Square, [pr], [sr])
                    act(KTN[:, k0:k0 + n], pt[:, 0:n], AF.Identity, [pr, R["VEC"]], [hr["K"]], scale=vcol(l, 171))
                    yield
                    for c in range(n // 128):
                        kc = kg * 4 + c
                        mm(rkp[:, kc:kc + 1], sq[:, c * 128:(c + 1) * 128], ONESB[:, 0:1], True, False, [sr, R["ONES"]], rkr)
                        mm(rkp[:, kc:kc + 1], HX[0:64, SQKR0 + kc * 128:SQKR0 + (kc + 1) * 128], ONESB[0:64, 0:1], False, True,
                           [R["KRB"], R["ONES"]], rkr)
                    yield
                act(RK[:, 0:18], rkp[:, 0:18], AF.Ln, [rkr, R["ONES"]], [hr["K"]], bias=EPST[:, 0:1], scale=1.0 / 192)
                act(RK[:, 0:18], RK[:, 0:18], AF.Exp, [hr["K"]], [hr["K"]], scale=-0.5)
                ts(RK[:, 0:18], RK[:, 0:18], ATTN_SCALE, None, ALU.mult, None, [hr["K"]], [hr["K"]])
                yield
                for kc4 in range(0, 18, 4):
                    cnt_ = min(4, 18 - kc4)
                    pt, pr = psp()
                    for i in range(cnt_):
                        kc = kc4 + i
                        for k in range(4):
                            mm(pt[:, i * 128:(i + 1) * 128], U1[:, CKV0 + k * NK + kc * 128:CKV0 + k * NK + kc * 128 + 128],
                               wb[:, k * 256 + 128:k * 256 + 256], k == 0, k == 3, [R["CKV"]], pr)
                    yield
                    act(VV[:, kc4 * 128:(kc4 + cnt_) * 128], pt[:, 0:cnt_ * 128], AF.Copy, [pr], [hr["V"]])
                    yield
                for gi, (t0, t1, seg) in enumerate(TGS):
                    if gi not in tgs:
                        continue
                    n = t1 - t0
                    pa, par_ = psp()
                    pb, pbr = psp()
                    for k in range(4):
                        mm(pa[:, 0:n], wb[:, 1024 + k * 192:1024 + k * 192 + 128], U1[:, k * T + t0:k * T + t1],
                           k == 0, k == 3, [R["CQ"]], par_)
                    for k in range(4):
                        mm(pb[0:64, 0:n], wb[:, 1024 + k * 192 + 128:1024 + k * 192 + 192], U1[:, k * T + t0:k * T + t1],
                           k == 0, k == 3, [R["CQ"]], pbr)
                    yield
                    sqa, sar = sqb()
                    act(sqa[:, 0:n], pa[:, 0:n], AF.Square, [par_], [sar])
                    sqb_, sbr = sqb()
                    act(sqb_[0:64, 0:n], pb[0:64, 0:n], AF.Square, [pbr], [sbr])
                    yield
                    pd, pdr = psp()
                    mm(pd[:, 0:n], ONESB[:], sqa[:, 0:n], True, False, [sar, R["ONES"]], pdr)
                    mm(pd[:, 0:n], ONESB[0:64, :], sqb_[0:64, 0:n], False, True, [sbr, R["ONES"]], pdr)
                    yield
                    rt, rr = rstd_from(pd[:, 0:n], n, 192, pdr)
                    yield
                    stt(QTN[:, t0:t1], pa[:, 0:n], vcol(l, 168), rt[:, 0:n], ALU.mult, ALU.mult, [par_, rr, R["VEC"]], [hr["Q"]])
                    bg, bgr = tf()
                    act(bg[0:64, 0:n], pb[0:64, 0:n], AF.Identity, [pbr, R["VEC"]], [bgr], scale=vcol(l, 169, 64))
                    tt(bg[0:64, 0:n], bg[0:64, 0:n], ROPEQ[:, t0:t1], ALU.mult, [bgr, R["ROPEQ"]], [bgr])
                    yield
                    pc, pcr = psp()
                    for k in range(4):
                        mm(pc[0:64, 0:n], wb[:, 1792 + k * 64:1792 + k * 64 + 64], U1[:, k * T + t0:k * T + t1],
                           k == 0, k == 3, [R["CQ"]], pcr)
                    yield
                    cg, cgr = tf()
                    act(cg[0:64, 0:n], pc[0:64, 0:n], AF.Identity, [pcr, R["VEC"]], [cgr], scale=vcol(l, 170, 64))
                    tt(cg[0:64, 0:n], cg[0:64, 0:n], ROPEQ[:, T + t0:T + t1], ALU.mult, [cgr, R["ROPEQ"]], [cgr])
                    tt(bg[0:64, 0:n], bg[0:64, 0:n], cg[0:64, 0:n], ALU.add, [bgr, cgr], [bgr])
                    tt(QTR[:, t0:t1], bg[0:64, 0:n], rt[0:64, 0:n], ALU.mult, [bgr, rr], [hr["Q"]])
                    yield

            def spv(h, gen):
                tl = tiles(h)
                KTN, KTR, VV, QTN, QTR, hr = tl["KTN"], tl["KTR"], tl["VV"], tl["QTN"], tl["QTR"], tl["hr"]
                RK = tl["RK"]
                it = 0
                for gi, (t0, t1, seg) in enumerate(TGS):
                    if gi not in tgs:
                        continue
                    n = t1 - t0
                    kcs = list(range(18)) if seg == 0 else [16, 17]
                    po, por = PSt[2], PS_R[2]
                    pss, psr = PSt[3], PS_R[3]

                    def smm(i):
                        kc = kcs[i]
                        sp_, spr = PSt[i % 2], PS_R[i % 2]
                        mm(sp_[:, 0:n], KTN[:, kc * 128:(kc + 1) * 128], QTN[:, t0:t1], True, False, [hr["K"], hr["Q"]], spr)
                        mm(sp_[:, 0:n], KTR[:, kc * 128:(kc + 1) * 128], QTR[:, t0:t1], False, True, [R["KRB"], hr["Q"]], spr)
                    smm(0)
                    for i in range(len(kcs)):
                        if i + 1 < len(kcs):
                            smm(i + 1)
                        kc = kcs[i]
                        sp_, spr = PSt[i % 2], PS_R[i % 2]
                        pt_, ptr = tb()
                        act(pt_[:, 0:n], sp_[:, 0:n], AF.Exp, [spr, hr["K"]], [ptr], scale=RK[:, kc:kc + 1])
                        mm(po[:, 0:n], VV[:, kc * 128:(kc + 1) * 128], pt_[:, 0:n], i == 0, i == len(kcs) - 1, [hr["V"], ptr], por)
                        mm(pss[:, 0:n], ONESB[:], pt_[:, 0:n], i == 0, i == len(kcs) - 1, [R["ONES"], ptr], psr)
                        it += 1
                        next(gen, None)
                    rc, rcr = tf()
                    act(rc[:, 0:n], pss[:, 0:n], AF.Ln, [psr], [rcr])
                    act(rc[:, 0:n], rc[:, 0:n], AF.Exp, [rcr], [rcr], scale=-1.0)
                    o_, orr = tb()
                    tt(o_[:, 0:n], po[:, 0:n], rc[:, 0:n], ALU.mult, [por, rcr], [orr])
                    ch = 16 + h
                    P.dma("pool", [(abo_d[ch * 128:(ch + 1) * 128, t0:t1], o_[:, 0:n])],
                          reads=[orr], writes=[ABO_R[ch][gi]], owner=orr)
                for _ in gen:
                    pass

            def hspecs(h):
                return [(wview(w_ukv_d[l], 0, 4, h * 256, 256), 0, 4, 256),
                        (wview(w_uq_d[l], 0, 4, h * 192, 192), 1024, 4, 192)]

            def first(wb):
                for _ in prep_steps(0, wb):
                    pass
            JOBSP.append(Job(hspecs(0), first, extra_cast=xcast))
            for h in range(16):
                if h < 15:
                    JOBSP.append(Job(hspecs(h + 1), lambda wb, h=h: spv(h, prep_steps(h + 1, wb)), extra_cast=xcast))
                else:
                    JOBSP.append(Job(None, lambda wb, h=h: spv(h, iter(()))))

        def merge_phase(l, tgs):
            for gi, (t0, t1, seg) in enumerate(TGS):
                if gi not in tgs:
                    continue
                n = t1 - t0

                def load(gi=gi, t0=t0, t1=t1, n=n):
                    av = abo_d.rearrange("(c p) t -> p c t", p=128)
                    for q in range(4):
                        P.dma("sp", [(U1[:, q * 8 * 512:(q + 1) * 8 * 512].rearrange("p (c t) -> p c t", c=8)[:, :, 0:n],
                                      av[:, q * 8:(q + 1) * 8, t0:t1])],
                              reads=[ABO_R[c][gi] for c in range(q * 8, q * 8 + 8)], writes=[R["ABO"]], owner=R["ABO"])
                simple(load)
                for j in range(16):
                    first_tg = gi == tgs[0]

                    def comp(wb, j=j, gi=gi, t0=t0, t1=t1, n=n, seg=seg, first_tg=first_tg):
                        if first_tg:
                            P.dma("pool", [(wcache_d[j * 128:(j + 1) * 128, :], wb[:, 0:4096])], reads=[CUR["wb"]],
                                  writes=[WC_R[j]], owner=CUR["wb"])
                        g_, gr = gt()
                        gv = gsc_d.rearrange("(b j p) t -> p b j t", p=128, b=3)
                        P.dma("sp", [(g_[:].rearrange("p (b t) -> p b t", b=3)[:, :, 0:n], gv[:, :, j, t0:t1])],
                              reads=[GSC_R[b * 16 + j][gi] for b in range(3)], writes=[gr], owner=gr)
                        pa, par_ = psm()
                        for k in range(8):
                            mm(pa[:, 0:n], wb[:, k * 128:(k + 1) * 128], U1[:, k * 512:k * 512 + n], k == 0, k == 7, [R["ABO"]], par_)
                        pb, pbr = psm()
                        for k in range(8):
                            mm(pb[:, 0:n], wb[:, 1024 + k * 128:1024 + (k + 1) * 128], U1[:, (8 + k) * 512:(8 + k) * 512 + n],
                               k == 0, k == 7, [R["ABO"]], pbr)
                        pc, pcr = psm()
                        for k in range(16):
                            mm(pc[:, 0:n], wb[:, 2048 + k * 128:2048 + (k + 1) * 128], U1[:, (16 + k) * 512:(16 + k) * 512 + n],
                               k == 0, k == 15, [R["ABO"]], pcr)
                        m1, m1r = tf()
                        tt(m1[:, 0:n], pa[:, 0:n], g_[:, 0:n], ALU.mult, [par_, gr], [m1r])
                        m2, m2r = tf()
                        tt(m2[:, 0:n], pb[:, 0:n], g_[:, 512:512 + n], ALU.mult, [pbr, gr], [m2r])
                        tt(m1[:, 0:n], m1[:, 0:n], m2[:, 0:n], ALU.add, [m1r, m2r], [m1r])
                        m3, m3r = tf()
                        tt(m3[:, 0:n], pc[:, 0:n], g_[:, 1024:1024 + n], ALU.mult, [pcr, gr], [m3r])
                        c0 = hxc(t0)
                        tt(HX[:, j * HXW + c0:j * HXW + c0 + n], m1[:, 0:n], m3[:, 0:n], ALU.add, [m1r, m3r], [HX_R[gi]])
                    if first_tg:
                        JOBSP.append(Job([(wview(w_ba_d[l], 0, 8, j * 128, 128), 0, 8, 128),
                                          (wview(w_bb_d[l], 0, 8, j * 128, 128), 1024, 8, 128),
                                          (wview(w_bc_d[l], 0, 16, j * 128, 128), 2048, 16, 128)], comp))
                    else:
                        JOBSP.append(Job([], comp, bf16_src=(wcache_d[j * 128:(j + 1) * 128, :], 4096, WC_R[j])))
            for b in range(8):
                def comp(wb, b=b):
                    for m in range(2):
                        i = 2 * b + m
                        for gi, (t0, t1, seg) in enumerate(TGS):
                            if gi not in tgs:
                                continue
                            n = t1 - t0
                            pt, pr = psm()
                            hx_mm(pt[:, 0:n], wb, 0, 256, m * 128, 128, hxc(t0), n, pr, [HX_R[gi]])
                            xk, xr = tf()
                            P.dma("sp", [(xk[:, 0:n], xs_view(i, t0, t1))], reads=[XS_R[i][gi]], writes=[xr], owner=xr)
                            xn, xnr = tf()
                            c = l * 192 + 2 * 32 + seg * 16 + i
                            stt(xn[:, 0:n], pt[:, 0:n], PAR[:, c:c + 1], xk[:, 0:n], ALU.mult, ALU.add, [pr, xr, PAR_R[l]], [xnr])
                            DEFER.append(lambda xn=xn, xnr=xnr, i=i, t0=t0, t1=t1, gi=gi, n=n:
                                         P.dma("pool", [(xs_view(i, t0, t1), xn[:, 0:n])], reads=[xnr], writes=[XS_R[i][gi]], owner=xnr))
                JOBSP.append(Job([(wview(w_out_d[l], 0, KC, b * 256, 256), 0, KC, 256)], comp))

        def ffn_phase(l, tgs, last):
            NQ, QC = 4, 11
            fc = 174
            for q in range(NQ):
                for jj in range(QC):
                    j = q * QC + jj

                    def comp(wb, j=j, jj=jj):
                        UE, CVf, SL = FP[:, 0:1026], FP[:, 1040:2064], FP[:, 2080:3104]
                        UEc, CVc, SLc = FP[:, 3120:3378], FP[:, 3400:3656], FP[:, 3700:3956]
                        pus, pvs = [], []
                        for gi in (0, 1):
                            t0, t1, seg = TGS[gi]
                            pu, pur = psm()
                            hx_mm(pu[:, 0:512], wb, 0, 128, 0, 128, hxc(t0), 512, pur, [HX_R[gi]])
                            pus.append((pu, pur))
                        ph, phr = psa()
                        for k in range(KC):
                            mm(ph[:, 0:2], wb[:, k * 128:k * 128 + 128], HX[:, k * HXW + 7:k * HXW + 1033:1025],
                               k == 0, k == KC - 1, [R["HXH"]], phr)
                        for gi in (0, 1):
                            t0, t1, seg = TGS[gi]
                            pv, pvr = psm()
                            hx_mm(pv[:, 0:512], wb, 2048, 128, 0, 128, hxc(t0), 512, pvr, [HX_R[gi]])
                            pvs.append((pv, pvr))
                        act(UE[:, 1:513], pus[0][0][:, 0:512], AF.Copy, [pus[0][1]], [R["ZA"]])
                        act(UE[:, 513:1025], pus[1][0][:, 0:512], AF.Copy, [pus[1][1]], [R["ZA"]])
                        act(UE[:, 0:1026:1025], ph[:, 0:2], AF.Copy, [phr], [R["ZA"]])
                        ts(CVf[:, 0:1024], UE[:, 1:1025], vcol(l, fc + 44 + j), None, ALU.mult, None, [R["ZA"], R["VEC"]], [R["ZB"]])
                        stt(CVf[:, 0:1024], UE[:, 0:1024], vcol(l, fc + j), CVf[:, 0:1024], ALU.mult, ALU.add,
                            [R["ZA"], R["VEC"], R["ZB"]], [R["ZB"]])
                        stt(CVf[:, 0:1024], UE[:, 2:1026], vcol(l, fc + 88 + j), CVf[:, 0:1024], ALU.mult, ALU.add,
                            [R["ZA"], R["VEC"], R["ZB"]], [R["ZB"]])
                        act(SL[:, 0:1024], CVf[:, 0:1024], AF.Silu, [R["ZB"]], [R["ZC"]])
                        for gi in (0, 1):
                            t0, t1, seg = TGS[gi]
                            tt(U1[:, jj * T + t0:jj * T + t1], SL[:, t0:t1], pvs[gi][0][:, 0:512], ALU.mult,
                               [R["ZC"], pvs[gi][1]], [R["FF"]])
                        if 2 in tgs:
                            t0, t1, seg = TGS[2]
                            pu, pur = psm()
                            hx_mm(pu[:, 0:256], wb, 0, 128, 0, 128, hxc(t0), 256, pur, [HX_R[2]])
                            pv, pvr = psm()
                            hx_mm(pv[:, 0:256], wb, 2048, 128, 0, 128, hxc(t0), 256, pvr, [HX_R[2]])
                            memset(UEc[:, 0:258:257], 0.0, [R["ZD"]], eng="dve")
                            act(UEc[:, 1:257], pu[:, 0:256], AF.Copy, [pur], [R["ZD"]])
                            ts(CVc[:, 0:256], UEc[:, 1:257], vcol(l, fc + 44 + j), None, ALU.mult, None, [R["ZD"], R["VEC"]], [R["ZD"]])
                            stt(CVc[:, 0:256], UEc[:, 0:256], vcol(l, fc + j), CVc[:, 0:256], ALU.mult, ALU.add,
                                [R["ZD"], R["VEC"]], [R["ZD"]])
                            stt(CVc[:, 0:256], UEc[:, 2:258], vcol(l, fc + 88 + j), CVc[:, 0:256], ALU.mult, ALU.add,
                                [R["ZD"], R["VEC"]], [R["ZD"]])
                            act(SLc[:, 0:256], CVc[:, 0:256], AF.Silu, [R["ZD"]], [R["ZD"]])
                            tt(U1[:, jj * T + t0:jj * T + t1], SLc[:, 0:256], pv[:, 0:256], ALU.mult, [R["ZD"], pvr], [R["FF"]])
                    JOBSP.append(Job([(wview(w_up_d[l], 0, KC, j * 128, 128), 0, KC, 128),
                                      (wview(w_up_d[l], 0, KC, DFF + j * 128, 128), 2048, KC, 128)], comp))
                for ip in range(8):
                    def comp(wb, ip=ip, q=q):
                        for m in range(2):
                            i = 2 * ip + m
                            for gi, (t0, t1, seg) in enumerate(TGS):
                                if gi not in tgs:
                                    continue
                                n = t1 - t0
                                pt, pr = psm()
                                for k in range(QC):
                                    mm(pt[:, 0:n], wb[:, k * 256 + m * 128:k * 256 + m * 128 + 128], U1[:, k * T + t0:k * T + t1],
                                       k == 0, k == QC - 1, [R["FF"]], pr)
                                xk, xr = tf()
                                P.dma("sp", [(xk[:, 0:n], xs_view(i, t0, t1))], reads=[XS_R[i][gi]], writes=[xr], owner=xr)
                                xn, xnr = tf()
                                c = l * 192 + 5 * 32 + seg * 16 + i
                                stt(xn[:, 0:n], pt[:, 0:n], PAR[:, c:c + 1], xk[:, 0:n], ALU.mult, ALU.add, [pr, xr, PAR_R[l]], [xnr])
                                if last and q == NQ - 1:
                                    OUT_TOKS.append(P.dma("pool", [(outT_d[i * 128:(i + 1) * 128, t0:t1], xn[:, 0:n])],
                                                          reads=[xnr], writes=[], owner=xnr))
                                else:
                                    P.dma("pool", [(xs_view(i, t0, t1), xn[:, 0:n])], reads=[xnr], writes=[XS_R[i][gi]], owner=xnr)
                    JOBSP.append(Job([(wview(w_dn_d[l], q * QC * 128, QC, ip * 256, 256), 0, QC, 256)], comp))

        def spread(main, extra):
            nw = sum(1 for j in main if j.specs)
            out, k, seen = [], 0, 0
            for j in main:
                out.append(j)
                if j.specs:
                    seen += 1
                    while k < len(extra) and (k + 1) * nw <= seen * len(extra):
                        out.append(extra[k])
                        k += 1
            out.extend(extra[k:])
            return out

        def collect(fn):
            lst = []
            SINK.append(lst)
            fn()
            SINK.pop()
            return lst

        for l in range(nlayers):
            last = l == nlayers - 1
            full = (0, 1, 2)
            tg_main = (0, 1) if last else full
            HSALL = [HS_R[i][n_] for i in range(2) for n_ in ("K", "V", "Q")]

            def part_a(l=l, tg_main=tg_main, full=full):
                b1_phase(l)
                inherit([R["POOLED"]], [R["FF"], R["ABO"], R["CQ"], R["CKV"]])
                inherit([R["ZA"], R["ZB"], R["ZC"], R["ZD"]], [R["KR"], R["SSR"]])
                pool_phase(l, tg_main)
                conv_phase(l, tg_main)
                gates_phase(l, tg_main)
                inherit([R["CQ"], R["CKV"]], [R["POOLED"], R["FF"], R["ABO"]])
                q_phase(l, tg_main)
                inherit([R["KR"], R["SSR"]], [R["ZA"], R["ZB"], R["ZC"], R["ZD"]])
                inherit([R["KRB"]], HX_R + [R["HXH"]])
                kv_prep(l)
                inherit(HSALL, HX_R + [R["HXH"]])
                attn_phase(l, tg_main)
                inherit(HX_R + [R["HXH"]], HSALL + [R["KRB"]])
                inherit([R["ABO"]], [R["CQ"], R["CKV"], R["POOLED"], R["FF"]])

            def part_b(l=l, tg_main=tg_main, last=last):
                merge_phase(l, tg_main)
                exchange_edges(1)
                norm_phase(l, 1, tg_main, 1)
                inherit([R["FF"]], [R["ABO"], R["CQ"], R["CKV"], R["POOLED"]])
                inherit([R["ZA"], R["ZB"], R["ZC"], R["ZD"]], [R["KR"], R["SSR"]])
                ffn_phase(l, tg_main, last)

            if l == 0:
                mods_blocks(0, 0, 8)
                mods_gather(0, [0, 1])
                mods_final(0, 0)
            exchange_edges(8)
            norm_phase(l, 0, full, 8)
            ja = collect(part_a)
            if l == 0:
                JOBS.extend(spread(ja, collect(lambda: mods_blocks(0, 8, 24))))
                mods_gather(0, [2, 3, 4, 5])
                mods_final(0, 1)
            else:
                JOBS.extend(ja)
            jb = collect(part_b)
            if l == 0 and nlayers > 1:
                JOBS.extend(spread(jb, collect(lambda: mods_blocks(1, 0, 24))))
                mods_gather(1, [0, 1, 2, 3, 4, 5])
                mods_final(1, 0)
                mods_final(1, 1)
            else:
                JOBS.extend(jb)
            print('layer', l, 'jobs so far', len(JOBS))

        if max_jobs is not None:
            JOBS = JOBS[:max_jobs]
        Wj = [j for j in JOBS if j.specs or j.bf16_src is not None]

        def load(i):
            b = i % 2
            if Wj[i].bf16_src is not None:
                return
            pairs = []
            for (src, off, kc, ncols) in Wj[i].specs:
                pairs.append((WS[b][:, off:off + kc * ncols].rearrange("p (k c) -> p k c", k=kc), src))
            P.dma("sp", pairs, writes=[WS_R[b]], owner=WS_R[b])

        def cast(i):
            b = i % 2
            if Wj[i].bf16_src is not None:
                src, nn, sres = Wj[i].bf16_src
                P.dma("sp", [(WB[b][:, 0:nn], src)], reads=[sres], writes=[WB_R[b]], owner=WB_R[b])
                return
            tot = max(off + kc * ncols for (_, off, kc, ncols) in Wj[i].specs)
            cp(WB[b][:, 0:tot], WS[b][:, 0:tot], [WS_R[b]], [WB_R[b]], eng=CAST_ENG)
            if Wj[i].extra_cast is not None:
                for (o, s) in Wj[i].extra_cast(WS[b], WB[b]):
                    cp(o, s, [WS_R[b]], [WB_R[b]], eng=CAST_ENG)

        if len(Wj) > 0:
            load(0)
        if len(Wj) > 1:
            load(1)
        if len(Wj) > 0:
            cast(0)
        wi_ = 0
        for job in JOBS:
            if job.specs or job.bf16_src is not None:
                b = wi_ % 2
                if wi_ + 1 < len(Wj):
                    cast(wi_ + 1)
                if wi_ + 2 < len(Wj):
                    load(wi_ + 2)
                CUR["wb"] = WB_R[b]
                job.compute(WB[b])
                CUR["wb"] = None
                wi_ += 1
            else:
                job.compute(None)
        for tok in OUT_TOKS:
            P.wait_tok("pool", tok)
        if debug:
            P.drain_all("pool")
        print("streams", {k: len(v) for k, v in P.streams.items()}, "jobs", len(JOBS), "sems", P.nsem)
        P.finalize(block)
    return nc


_PERM = np.concatenate([np.arange(16, 32), np.arange(0, 16), np.arange(48, 64), np.arange(32, 48)])
_NC_CACHE = {}


def _fm(v, ncol):
    return np.ascontiguousarray(np.asarray(v, np.float32).reshape(ncol, 128).T)


def _rope_tables(row, col):
    inv = (10000.0 ** (-np.arange(0, 32, 2, dtype=np.float32) / np.float32(32))).astype(np.float32)
    ar = (row[None, :].astype(np.float32) * inv[:, None]).astype(np.float32)
    ac = (col[None, :].astype(np.float32) * inv[:, None]).astype(np.float32)
    cos = np.concatenate([np.cos(ar), np.cos(ar), np.cos(ac), np.cos(ac)], axis=0).astype(np.float32)
    ssin = np.concatenate([-np.sin(ar), np.sin(ar), -np.sin(ac), np.sin(ac)], axis=0).astype(np.float32)
    return cos, ssin


def _prepare(inp):
    g = {k: np.asarray(v) for k, v in inp.items()}
    x, c, ctx, c_ctx = g["x"], g["c"], g["ctx"], g["c_ctx"]
    vecs = np.zeros((2, 128, NV), np.float32)
    for l in range(2):
        v = vecs[l]
        v[:, 0:16] = _fm(g["norm1_g"][l], 16)
        v[:, 16:32] = _fm(g["norm2_g"][l], 16)
        v[:, 32:128] = _fm(g["b_mod"][l], 96)
        v[:, 128:136] = _fm(g["pool_scale"][l], 8)
        for t in range(3):
            v[:, 136 + t * 8:136 + t * 8 + 8] = _fm(g["conv_w"][l, t], 8)
        v[:, 160:164] = _fm(g["q_lora_g"][l], 4)
        v[:, 164:168] = _fm(g["kv_lora_g"][l], 4)
        qh, kh = g["q_head_g"][l], g["k_head_g"][l]
        v[:, 168] = qh[0:128]
        v[0:64, 169] = qh[128:192]
        v[0:64, 170] = qh[128:192][_PERM]
        v[:, 171] = kh[0:128]
        v[0:64, 172] = kh[128:192]
        v[0:64, 173] = kh[128:192][_PERM]
        for t in range(3):
            v[:, 174 + t * 44:174 + (t + 1) * 44] = _fm(g["ffn_conv"][l, t], 44)
    tl = np.arange(2048)
    ck, sk = _rope_tables(tl // 64, tl % 64)
    ropek = np.zeros((2, 64, NK), np.float32)
    ropek[0, :, :2048] = ck
    ropek[0, :, 2048:] = 1.0
    ropek[1, :, :2048] = sk
    shared = {
        "vecs": vecs, "ropek": ropek,
        "w_in": np.ascontiguousarray(g["w_in"], np.float32),
        "pool_w": np.ascontiguousarray(g["pool_w"], np.float32).reshape(2, 1024, 256),
        "w_uq": np.ascontiguousarray(g["w_uq"], np.float32).reshape(2, 512, 3072),
        "w_ukv": np.ascontiguousarray(g["w_ukv"], np.float32).reshape(2, 512, 4096),
        "w_branch_a": np.ascontiguousarray(g["w_branch_a"], np.float32),
        "w_branch_b": np.ascontiguousarray(g["w_branch_b"], np.float32),
        "w_branch_c": np.ascontiguousarray(g["w_branch_c"], np.float32),
        "w_out": np.ascontiguousarray(g["w_out"], np.float32),
        "w_ffn_up": np.ascontiguousarray(g["w_ffn_up"], np.float32),
        "w_ffn_down": np.ascontiguousarray(g["w_ffn_down"], np.float32),
    }
    in_maps = []
    for r in range(8):
        b, half = r // 2, r % 2
        s0 = half * 1024
        m = dict(shared)
        m["xT"] = np.ascontiguousarray(x[b, s0:s0 + 1024, :].T, np.float32)
        m["ctxT"] = np.ascontiguousarray(ctx[b].T, np.float32)
        cv = np.zeros((128, 80), np.float32)
        for j in range(4):
            cv[:, j * 16:(j + 1) * 16] = _fm(c[j], 16)
        cv[:, 64:80] = _fm(c_ctx, 16)
        m["cvec"] = cv
        oh = np.zeros((128, 4), np.float32)
        oh[:, b] = 1.0
        m["oh"] = oh
        m["w_mod"] = np.ascontiguousarray(np.concatenate(
            [g["w_mod"][:, :, wh * 2048 + half * 1024:wh * 2048 + (half + 1) * 1024] for wh in range(6)], axis=2), np.float32)
        rq = np.zeros((2, 64, T), np.float32)
        rq[0, :, :1024] = ck[:, s0:s0 + 1024]
        rq[0, :, 1024:] = 1.0
        rq[1, :, :1024] = sk[:, s0:s0 + 1024]
        m["ropeq"] = rq
        rcb = np.zeros((128, 128), np.float32)
        for seg, (L, starts) in enumerate(((2048, (s0, s0 + 1016)), (256, (0, 248)))):
            for wi_, w in enumerate(POOL_W):
                h = w // 2
                for side in range(2):
                    for i in range(8):
                        t = starts[side] + i
                        cntv = min(t + h, L) - max(t - h, 0)
                        rcb[:, (seg * 4 + wi_) * 16 + side * 8 + i] = 1.0 / cntv
        m["rcb"] = rcb
        mk = np.zeros((128, 2), np.float32)
        mk[:, 0] = 1.0 if half == 1 else 0.0
        mk[:, 1] = 1.0 if half == 0 else 0.0
        m["mask"] = mk
        in_maps.append(m)
    return in_maps


def kernel(**inp):
    in_maps = _prepare(inp)
    if "nc" not in _NC_CACHE:
        _NC_CACHE["nc"] = build_program(2)
    res = run_bass_kernel_spmd(_NC_CACHE["nc"], in_maps, core_ids=list(range(8)))
    out = np.zeros((4, 2048, 2048), np.float32)
    for r in range(8):
        b, half = r // 2, r % 2
        out[b, half * 1024:(half + 1) * 1024, :] = np.asarray(res.results[r]["outT"], np.float32).T
    return out
```

```python
import numpy as np
from contextlib import ExitStack
import concourse.bass as bass
import concourse.mybir as mybir
from concourse.bass_utils import run_bass_kernel_spmd

F32 = mybir.dt.float32
BF16 = mybir.dt.bfloat16
ALU = mybir.AluOpType
AF = mybir.ActivationFunctionType

SEM_EPOCH = 20000
CAST_ENG = "dve"
CUR = {"wb": None}
D = 2048
KC = 16
T = 1280
NK = 2304
HXW = 1296
DFF = 5632
NV = 306
ATTN_SCALE = 192.0 ** -0.5
EPS = 1e-6
POOL_W = (2, 4, 8, 16)
TGS = [(0, 512, 0), (512, 1024, 0), (1024, 1280, 1)]


def hxc(t):
    return 8 + t if t < 1024 else 16 + t


class Res:
    __slots__ = ("name", "lw", "rd", "dsem", "dcnt")

    ALL = []

    def __init__(self, name):
        self.name = name
        self.lw = None
        self.rd = {}
        self.dsem = None
        self.dcnt = 0
        Res.ALL.append(self)


class Prog:
    ENGS = ("pe", "act", "dve", "pool", "sp")

    def __init__(self, nc):
        self.nc = nc
        self.streams = {e: [] for e in self.ENGS}
        self.esem = {e: None for e in self.ENGS}
        self.ecnt = {e: 0 for e in self.ENGS}
        self.own = {e: set() for e in self.ENGS}
        self.waited = {e: {} for e in self.ENGS}
        self.nsem = 0

    def new_sem(self, name):
        self.nsem += 1
        return self.nc.alloc_semaphore(name=f"{name}_{self.nsem}")

    def _need(self, eng, deps):
        w = self.waited[eng]
        own = self.own[eng]
        best = {}
        for d in deps:
            tok, raw = d
            if tok is None:
                continue
            sem, val = tok
            k = id(sem)
            if k in own and (eng == "pe" or not raw):
                continue
            if w.get(k, 0) >= val:
                continue
            if k not in best or best[k][1] < val:
                best[k] = (sem, val)
        for k, (sem, val) in best.items():
            w[k] = val
            self.streams[eng].append(("wait", sem, val))

    @staticmethod
    def _deps(reads, writes):
        deps = []
        for r in reads:
            deps.append((r.lw, True))
        for r in writes:
            deps.append((r.lw, False))
            for t in r.rd.values():
                deps.append((t, False))
        return deps

    def op(self, eng, fn, reads=(), writes=()):
        self._need(eng, self._deps(reads, writes))
        if self.esem[eng] is None or self.ecnt[eng] >= SEM_EPOCH:
            self.esem[eng] = self.new_sem(eng)
            self.own[eng].add(id(self.esem[eng]))
            self.ecnt[eng] = 0
        sem = self.esem[eng]
        self.ecnt[eng] += 1
        tok = (sem, self.ecnt[eng])
        self.streams[eng].append(("op", fn, sem, 1))
        for r in reads:
            r.rd[eng] = tok
        for r in writes:
            r.lw = tok
            r.rd = {}
        return tok

    def _owned(self, q, pairs_n, reads, writes, owner, inc, items):
        deps = self._deps(reads, writes)
        if owner.dsem is None or owner.dcnt >= SEM_EPOCH:
            owner.dsem = self.new_sem("d")
            owner.dcnt = 0
        elif owner.dcnt > 0:
            deps.append(((owner.dsem, owner.dcnt), True))
        self._need(q, deps)
        sem = owner.dsem
        for it in items:
            owner.dcnt += inc
            self.streams[q].append(it + (sem,))
        tok = (sem, owner.dcnt)
        for r in reads:
            r.rd[("d", id(sem))] = tok
        for r in writes:
            r.lw = tok
            r.rd = {}
        return tok

    def dma(self, q, pairs, reads=(), writes=(), owner=None, slow=False):
        return self._owned(q, len(pairs), reads, writes, owner, 16, [("dmas" if slow else "dma", o, i) for (o, i) in pairs])

    def coll(self, fn, reads, writes, owner):
        return self._owned("pool", 1, reads, writes, owner, 1, [("coll", fn, None)])

    def wait_tok(self, eng, tok):
        self._need(eng, [(tok, True)])

    def drain_all(self, eng):
        deps = []
        for e in self.ENGS:
            if self.esem[e] is not None and e != eng:
                deps.append(((self.esem[e], self.ecnt[e]), True))
        for r in Res.ALL:
            if r.dsem is not None and r.dcnt > 0:
                deps.append(((r.dsem, r.dcnt), True))
        self._need(eng, deps)

    def finalize(self, block):
        streams = self.streams

        def play(e, name):
            for it in streams[name]:
                if it[0] == "wait":
                    e.wait_ge(it[1], it[2])
                elif it[0] == "op":
                    it[1](e).then_inc(it[2], it[3])
                elif it[0] == "dma":
                    e.dma_start(out=it[1], in_=it[2]).then_inc(it[3], 16)
                elif it[0] == "dmas":
                    e.dma_start(out=it[1], in_=it[2], allow_slow_non_contiguous=True).then_inc(it[3], 16)
                else:
                    it[1](e).then_inc(it[3], 1)

        @block.tensor
        def _(e):
            play(e, "pe")

        @block.scalar
        def _(e):
            play(e, "act")

        @block.vector
        def _(e):
            play(e, "dve")

        @block.gpsimd
        def _(e):
            play(e, "pool")

        @block.sync
        def _(e):
            play(e, "sp")


class Job:
    def __init__(self, specs, compute, extra_cast=None, bf16_src=None):
        self.specs = specs
        self.compute = compute
        self.extra_cast = extra_cast
        self.bf16_src = bf16_src


def build_program(nlayers=2, rg=None, max_jobs=None, debug=False):
    Res.ALL = []
    nc = bass.Bass("TRN2", target_bir_lowering=False)

    def din(name, shape, dt=F32):
        return nc.dram_tensor(name, shape, dt, kind="ExternalInput").ap()

    def dint(name, shape, dt=F32):
        return nc.dram_tensor(name, shape, dt, kind="ExternalOutput" if (debug and name in ("xs", "gsc", "abo", "zkvc")) else "Internal").ap()

    xT_d = din("xT", [D, 1024])
    ctxT_d = din("ctxT", [D, 256])
    cvec_d = din("cvec", [128, 80])
    oh_d = din("oh", [128, 4])
    vecs_d = din("vecs", [2, 128, NV])
    ropeq_d = din("ropeq", [2, 64, T])
    ropek_d = din("ropek", [2, 64, NK])
    rcb_d = din("rcb", [128, 128])
    mask_d = din("mask", [128, 2])
    w_mod_d = din("w_mod", [2, D, 6144])
    w_in_d = din("w_in", [2, D, 11328])
    pool_w_d = din("pool_w", [2, 1024, 256])
    w_uq_d = din("w_uq", [2, 512, 3072])
    w_ukv_d = din("w_ukv", [2, 512, 4096])
    w_ba_d = din("w_branch_a", [2, 1024, D])
    w_bb_d = din("w_branch_b", [2, 1024, D])
    w_bc_d = din("w_branch_c", [2, D, D])
    w_out_d = din("w_out", [2, D, D])
    w_up_d = din("w_ffn_up", [2, D, 2 * DFF])
    w_dn_d = din("w_ffn_down", [2, DFF, D])
    outT_d = nc.dram_tensor("outT", [D, 1024], F32, kind="ExternalOutput").ap()

    xs_d = dint("xs", [D, T])
    gsc_d = dint("gsc", [3 * D, T], BF16)
    abo_d = dint("abo", [2 * D, T], BF16)
    ex1i_d = dint("ex1i", [D, 16])
    ex1o_d = dint("ex1o", [2 * D, 16])
    ex2i_d = [dint(f"ex2i{i}", [576, 512]) for i in range(2)]
    ex2o_d = [dint(f"ex2o{i}", [1152, 512]) for i in range(2)]
    zkvc_d = dint("zkvc", [576, 256])
    wcache_d = dint("wcache", [16 * 128, 4096], BF16)
    NR = 2
    CPR = 96 // NR
    modp_d = dint("modp", [2 * CPR * 128, 8])
    modo_d = dint("modo", [NR * 2 * CPR * 128, 8])
    RG = rg if rg is not None else [[0, 1], [2, 3], [4, 5], [6, 7]]
    RG_ALL = RG

    es = ExitStack()
    with es:
        def sb(name, shape, dt):
            return es.enter_context(nc.sbuf_tensor(name, shape, dt))

        VEC = sb("VEC", [128, 2 * NV], F32)
        CV = sb("CV", [128, 80], F32)
        OH = sb("OH", [128, 4], F32)
        MP = sb("MP", [128, 16], F32)
        MODF = sb("MODF", [128, 1536], F32)
        SCB = sb("SCB", [128, 80], BF16)
        MODT = sb("MODT", [128, 384], F32)
        PAR = sb("PAR", [128, 384], F32)
        ROPEQ = sb("ROPEQ", [64, 2 * T], F32)
        RCBT = sb("RCBT", [128, 128], F32)
        MASK = sb("MASK", [128, 2], F32)
        ONESF = sb("ONESF", [128, 128], F32)
        ONESB = sb("ONESB", [128, 128], BF16)
        EPST = sb("EPST", [128, 1], F32)
        EDG = sb("EDG", [128, 16, 16], F32)
        EDG2 = sb("EDG2", [128, 16, 16], F32)
        HX = sb("HX", [128, KC * HXW], BF16)
        U1 = sb("U1", [128, 16384], BF16)
        FP = sb("FP", [128, 4608], F32)
        WS = [sb(f"WS{i}", [128, 4096], F32) for i in range(2)]
        WB = [sb(f"WB{i}", [128, 4096], BF16) for i in range(2)]
        TFt = [sb(f"TF{i}", [128, 528], F32) for i in range(8)]
        TBt = [sb(f"TB{i}", [128, 512], BF16) for i in range(4)]
        RSt = [sb(f"RS{i}", [128, 512], F32) for i in range(2)]
        GTt = [sb(f"GT{i}", [128, 3 * 512], BF16) for i in range(2)]
        H4 = sb("H4", [128, 16], F32)
        RKt = [sb(f"RK{i}", [128, 18], F32) for i in range(2)]
        SQt = [sb(f"SQ{i}", [128, 512], BF16) for i in range(2)]
        PSt = [es.enter_context(nc.psum_tensor(f"PS{i}", [128, 512], F32)) for i in range(8)]
        block = es.enter_context(nc.Block())
        P = Prog(nc)

        R = {n: Res(n) for n in ("VEC", "CV", "SCB", "MODT", "PAR", "ROPEQ", "RCBT", "MASK", "ONES", "EDG", "EDG2",
                                 "HXH", "CQ", "CKV", "KR", "SSR", "ZA", "ZB", "ZC", "ZD", "ZQ", "POOLED", "ABO",
                                 "EX1I", "EX1O", "EX2I", "EX2O", "ZKVC", "H4", "DUMMY", "MG", "FF", "OH", "MP", "MODF", "MODP", "MODO", "KRB")}
        HX_R = [Res(f"HX{i}") for i in range(3)]
        PAR_R = [Res(f"PAR{i}") for i in range(2)]
        WC_R = [Res(f"WC{i}") for i in range(16)]
        MODT_R = [Res(f"MODT{i}") for i in range(2)]
        WS_R = [Res(f"WS{i}") for i in range(2)]
        WB_R = [Res(f"WB{i}") for i in range(2)]
        TF_R = [Res(f"TF{i}") for i in range(8)]
        TB_R = [Res(f"TB{i}") for i in range(4)]
        RS_R = [Res(f"RS{i}") for i in range(2)]
        GT_R = [Res(f"GT{i}") for i in range(2)]
        SQ_R = [Res(f"SQ{i}") for i in range(2)]
        PS_R = [Res(f"PS{i}") for i in range(8)]
        XS_R = [[Res(f"XS{c}_{g}") for g in range(3)] for c in range(KC)]
        GSC_R = [[Res(f"GSC{c}_{g}") for g in range(3)] for c in range(48)]
        ABO_R = [[Res(f"ABO{c}_{g}") for g in range(3)] for c in range(32)]
        HS_R = [{n: Res(f"{n}{i}") for n in ("K", "V", "Q")} for i in range(2)]
        OUT_TOKS = []
        class _Now:
            def append(self, fn):
                fn()

            def clear(self):
                pass

            def __iter__(self):
                return iter(())
        DEFER = _Now()
        cnt = {"tf": 0, "tb": 0, "rs": 0, "gt": 0, "main": 0, "aux": 0, "prep": 0, "sq": 0}

        def sqb():
            i = cnt["sq"] % 2
            cnt["sq"] += 1
            return SQt[i], SQ_R[i]

        def tf():
            i = cnt["tf"] % 8
            cnt["tf"] += 1
            return TFt[i], TF_R[i]

        def tb():
            i = cnt["tb"] % 4
            cnt["tb"] += 1
            return TBt[i], TB_R[i]

        def rs():
            i = cnt["rs"] % 2
            cnt["rs"] += 1
            return RSt[i], RS_R[i]

        def gt():
            i = cnt["gt"] % 2
            cnt["gt"] += 1
            return GTt[i], GT_R[i]

        def psm():
            i = cnt["main"] % 6
            cnt["main"] += 1
            return PSt[i], PS_R[i]

        def psa():
            i = 6 + cnt["aux"] % 2
            cnt["aux"] += 1
            return PSt[i], PS_R[i]

        def psp():
            i = 5 + cnt["prep"] % 3
            cnt["prep"] += 1
            return PSt[i], PS_R[i]

        def mm(ps, lhsT, rhs, start, stop, reads, pres):
            if CUR["wb"] is not None:
                reads = list(reads) + [CUR["wb"]]
            P.op("pe", lambda e: e.matmul(ps, lhsT=lhsT, rhs=rhs, start=start, stop=stop), reads=reads, writes=[pres])

        def inherit(news, olds):
            def go():
                for nw in news:
                    for o in olds:
                        if o.lw is not None:
                            nw.rd[("inh", id(o), "w")] = o.lw
                        for k, t in o.rd.items():
                            nw.rd[("inh", id(o), k)] = t
            simple(go)

        def act(out, in_, func, reads, writes, bias=None, scale=None):
            kw = {}
            if bias is not None:
                kw["bias"] = bias
            if scale is not None:
                kw["scale"] = scale
            P.op("act", lambda e: e.activation(out=out, in_=in_, func=func, **kw), reads=reads, writes=writes)

        def tt(out, in0, in1, op, reads, writes, eng="dve"):
            P.op(eng, lambda e: e.tensor_tensor(out=out, in0=in0, in1=in1, op=op), reads=reads, writes=writes)

        def ts(out, in0, s1, s2, op0, op1, reads, writes, eng="dve"):
            if s2 is None:
                P.op(eng, lambda e: e.tensor_scalar(out=out, in0=in0, scalar1=s1, scalar2=None, op0=op0), reads=reads, writes=writes)
            else:
                P.op(eng, lambda e: e.tensor_scalar(out=out, in0=in0, scalar1=s1, scalar2=s2, op0=op0, op1=op1), reads=reads, writes=writes)

        def stt(out, in0, scalar, in1, op0, op1, reads, writes, eng="dve"):
            P.op(eng, lambda e: e.scalar_tensor_tensor(out=out, in0=in0, scalar=scalar, in1=in1, op0=op0, op1=op1),
                 reads=reads, writes=writes)

        def cp(out, in_, reads, writes, eng="dve"):
            P.op(eng, lambda e: e.tensor_copy(out=out, in_=in_), reads=reads, writes=writes)

        def recip(out, in_, reads, writes):
            P.op("dve", lambda e: e.reciprocal(out=out, in_=in_), reads=reads, writes=writes)

        def memset(ap, val, writes, eng="pool"):
            P.op(eng, lambda e: e.memset(ap, val), writes=writes)

        def rstd_from(ps_ap, n, dim, pres, parts=128):
            rt, rr = rs()
            act(rt[0:parts, 0:n], ps_ap, AF.Ln, [pres, R["ONES"]], [rr], bias=EPST[0:parts, 0:1], scale=1.0 / dim)
            act(rt[0:parts, 0:n], rt[0:parts, 0:n], AF.Exp, [rr], [rr], scale=-0.5)
            return rt, rr

        def vcol(l, c, parts=128):
            return VEC[0:parts, l * NV + c: l * NV + c + 1]

        def xs_view(k, t0, t1):
            return xs_d[k * 128:(k + 1) * 128, t0:t1]

        def wview(w2d, r0, kc, c0, ncols):
            return w2d[r0:r0 + kc * 128, c0:c0 + ncols].rearrange("(k p) c -> p k c", p=128)

        JOBS = []
        SINK = [JOBS]

        class _JP:
            def append(self, j):
                SINK[-1].append(j)
        JOBSP = _JP()

        def simple(fn):
            SINK[-1].append(Job(None, lambda wb: fn()))

        def setup():
            for l in range(2):
                P.dma("sp", [(VEC[:, l * NV:(l + 1) * NV], vecs_d[l])], writes=[R["VEC"]], owner=R["VEC"])
            P.dma("sp", [(CV[:], cvec_d[:, :])], writes=[R["CV"]], owner=R["CV"])
            P.dma("sp", [(ROPEQ[:, 0:T], ropeq_d[0]), (ROPEQ[:, T:2 * T], ropeq_d[1])], writes=[R["ROPEQ"]], owner=R["ROPEQ"])
            P.dma("sp", [(RCBT[:], rcb_d[:, :])], writes=[R["RCBT"]], owner=R["RCBT"])
            P.dma("sp", [(MASK[:], mask_d[:, :])], writes=[R["MASK"]], owner=R["MASK"])
            memset(ONESF[:], 1.0, [R["ONES"]])
            memset(ONESB[:], 1.0, [R["ONES"]])
            memset(EPST[:], EPS, [R["ONES"]])
            memset(HX[:], 0.0, HX_R + [R["HXH"]])
            for j in range(5):
                act(SCB[:, j:80:5], CV[:, j * 16:(j + 1) * 16], AF.Silu, [R["CV"]], [R["SCB"]])
            P.dma("sp", [(OH[:], oh_d[:, :])], writes=[R["OH"]], owner=R["OH"])
            memset(MP[:], 0.0, [R["MP"]])
            memset(FP[:, 0:768], 0.0, [R["ZA"]])
            P.dma("pool", [(modp_d.rearrange("(p r) j -> p (r j)", p=128), FP[:, 0:768])], reads=[R["ZA"]], writes=[R["MODP"]],
                  owner=R["ZA"])
            allx = [XS_R[c][g] for c in range(KC) for g in range(3)]
            P.dma("sp", [(xs_d[:, 0:1024], xT_d[:, :]), (xs_d[:, 1024:1280], ctxT_d[:, :])], writes=allx, owner=R["DUMMY"])

        simple(setup)

        def exchange_edges(ncol):
            def go():
                allx0 = [XS_R[c][0] for c in range(KC)]
                allx1 = [XS_R[c][1] for c in range(KC)]
                xv = xs_d.rearrange("(k p) t -> p k t", p=128)
                P.dma("sp", [(EDG[:, :, 0:ncol], xv[:, :, 0:ncol]), (EDG[:, :, 8:8 + ncol], xv[:, :, 1024 - ncol:1024])],
                      reads=allx0 + allx1, writes=[R["EDG"]], owner=R["EDG"], slow=True)
                ev = ex1i_d.rearrange("(k p) t -> p k t", p=128)
                P.dma("pool", [(ev, EDG[:])], reads=[R["EDG"]], writes=[R["EX1I"]], owner=R["EDG"])
                P.coll(lambda e: e.collective_compute("AllGather", ALU.bypass, replica_groups=RG, ins=[ex1i_d], outs=[ex1o_d]),
                       reads=[R["EX1I"]], writes=[R["EX1O"]], owner=R["EX1O"])
                ov = ex1o_d.rearrange("(r k p) t -> r p k t", p=128, r=2)
                P.dma("sp", [(EDG2[:, :, 8 - ncol:8], ov[0][:, :, 8:8 + ncol]), (EDG2[:, :, 8:8 + ncol], ov[1][:, :, 0:ncol])],
                      reads=[R["EX1O"]], writes=[R["EDG2"]], owner=R["EDG2"], slow=True)
            simple(go)

        def norm_phase(l, which, tgs, nh):
            qa, qb = (0, 1) if which == 0 else (3, 4)

            def par(q, seg, k):
                c = l * 192 + q * 32 + seg * 16 + k
                return PAR[:, c:c + 1]

            def one_tg(t0, t1, seg, gi):
                n = t1 - t0
                c0 = hxc(t0)
                pt, pr = psm()
                for k in range(KC):
                    xk, xr = tf()
                    P.dma("sp", [(xk[:, 0:n], xs_view(k, t0, t1))], reads=[XS_R[k][gi]], writes=[xr], owner=xr)
                    sq, sr = tf()
                    act(sq[:, 0:n], xk[:, 0:n], AF.Square, [xr], [sr])
                    mm(pt[:, 0:n], ONESF[:], sq[:, 0:n], k == 0, k == KC - 1, [sr, R["ONES"]], pr)
                rt, rr = rstd_from(pt[:, 0:n], n, D, pr)
                for k in range(KC):
                    xk, xr = tf()
                    P.dma("sp", [(xk[:, 0:n], xs_view(k, t0, t1))], reads=[XS_R[k][gi]], writes=[xr], owner=xr)
                    tm, tr = tf()
                    tt(tm[:, 0:n], xk[:, 0:n], rt[:, 0:n], ALU.mult, [xr, rr], [tr])
                    act(HX[:, k * HXW + c0:k * HXW + c0 + n], tm[:, 0:n], AF.Identity, [tr, PAR_R[l]], [HX_R[gi]],
                        bias=par(qb, seg, k), scale=par(qa, seg, k))

            def halo():
                lo, hi = 8 - nh, 8 + nh
                n = 2 * nh
                pt, pr = psm()
                for k in range(KC):
                    sq, sr = tf()
                    act(sq[:, 0:n], EDG2[:, k, lo:hi], AF.Square, [R["EDG2"]], [sr])
                    mm(pt[:, 0:n], ONESF[:], sq[:, 0:n], k == 0, k == KC - 1, [sr, R["ONES"]], pr)
                rt, rr = rstd_from(pt[:, 0:n], n, D, pr)
                for k in range(KC):
                    tm, tr = tf()
                    tt(tm[:, 0:n], EDG2[:, k, lo:hi], rt[:, 0:n], ALU.mult, [R["EDG2"], rr], [tr])
                    t2, t2r = tf()
                    act(t2[:, 0:n], tm[:, 0:n], AF.Identity, [tr, PAR_R[l]], [t2r], bias=par(qb, 0, k), scale=par(qa, 0, k))
                    ts(HX[:, k * HXW + lo:k * HXW + 8], t2[:, 0:nh], MASK[:, 0:1], None, ALU.mult, None,
                       [t2r, R["MASK"]], [R["HXH"]])
                    ts(HX[:, k * HXW + 1032:k * HXW + 1032 + nh], t2[:, nh:n], MASK[:, 1:2], None, ALU.mult, None,
                       [t2r, R["MASK"]], [R["HXH"]])

            def go():
                for gi, (t0, t1, seg) in enumerate(TGS):
                    if gi in tgs:
                        one_tg(t0, t1, seg, gi)
                halo()
            simple(go)

        def mods_blocks(l, b0, b1):
            for b in range(b0, b1):
                def comp(wb, l=l, b=b):
                    pt, pr = psa()
                    for m in range(2):
                        for k in range(KC):
                            mm(pt[:, 8 * m:8 * m + 5], wb[:, k * 256 + m * 128:k * 256 + m * 128 + 128],
                               SCB[:, 5 * k:5 * k + 5], k == 0, k == KC - 1, [R["SCB"]], pr)
                    for m in range(2):
                        act(MP[:, 8 * m:8 * m + 5], pt[:, 8 * m:8 * m + 5], AF.Copy, [pr], [R["MP"]])
                    r0 = l * CPR * 128 + 2 * b * 128
                    P.dma("pool", [(modp_d[r0:r0 + 256, :].rearrange("(m p) j -> p m j", p=128),
                                    MP[:].rearrange("p (m j) -> p m j", m=2))],
                          reads=[R["MP"]], writes=[R["MODP"]], owner=R["MP"], slow=True)
                JOBSP.append(Job([(wview(w_mod_d[l], 0, KC, b * 256, 256), 0, KC, 256)], comp))

        def mods_gather(l, whs):
            def gather():
                P.coll(lambda e: e.collective_compute("AllGather", ALU.bypass, replica_groups=RG_ALL, ins=[modp_d], outs=[modo_d]),
                       reads=[R["MODP"]], writes=[R["MODO"]], owner=R["MODO"])
                ov = modo_d.rearrange("(r l c p) j -> r l p c j", r=NR, l=2, c=CPR, p=128)
                pairs = []
                for r in range(NR):
                    for wh in whs:
                        o0 = l * 768 + (wh * 16 + r * 8) * 8
                        pairs.append((MODF[:, o0:o0 + 64].rearrange("p (c j) -> p c j", c=8), ov[r][l][:, wh * 8:(wh + 1) * 8, :]))
                P.dma("sp", pairs, reads=[R["MODO"]], writes=[R["MODF"]], owner=R["MODF"], slow=True)
                mo = l * 192
                mf = l * 768
                for wh in whs:
                    a0, a1 = mo + 32 * wh, mo + 32 * wh + 32
                    f0, f1 = mf + 128 * wh, mf + 128 * wh + 128
                    cp(MODT[:, a0 + 1:a1:2], MODF[:, f0 + 4:f1:8], [R["MODF"]], [MODT_R[l]])
                    ts(MODT[:, a0:a1:2], MODF[:, f0:f1:8], OH[:, 0:1], None, ALU.mult, None, [R["MODF"], R["OH"]], [MODT_R[l]])
                    for j in range(1, 4):
                        stt(MODT[:, a0:a1:2], MODF[:, f0 + j:f1:8], OH[:, j:j + 1], MODT[:, a0:a1:2],
                            ALU.mult, ALU.add, [R["MODF"], R["OH"], MODT_R[l]], [MODT_R[l]])
            simple(gather)

        def mods_final(l, part):
            mo = l * 192
            po = l * 192

            def go():
                c0, c1 = (0, 64) if part == 0 else (64, 192)
                for w in range(2):
                    tt(MODT[:, mo + c0 + w:mo + c1:2], MODT[:, mo + c0 + w:mo + c1:2],
                       VEC[:, l * NV + 32 + c0 // 2:l * NV + 32 + c1 // 2], ALU.add, [MODT_R[l], R["VEC"]], [MODT_R[l]])
                for w in range(2):
                    for (q, wsc, gcol) in (((0, 1, 0),) if part == 0 else ((3, 4, 16),)):
                        stt(PAR[:, po + q * 32 + w * 16:po + q * 32 + w * 16 + 16], MODT[:, mo + 32 * wsc + w:mo + 32 * wsc + 32:2], 1.0,
                            VEC[:, l * NV + gcol:l * NV + gcol + 16], ALU.add, ALU.mult, [MODT_R[l], R["VEC"]], [PAR_R[l]])
                    for (q, wh) in (((1, 0),) if part == 0 else ((2, 2), (4, 3), (5, 5))):
                        cp(PAR[:, po + q * 32 + w * 16:po + q * 32 + w * 16 + 16], MODT[:, mo + 32 * wh + w:mo + 32 * wh + 32:2],
                           [MODT_R[l]], [PAR_R[l]])
            simple(go)

        def hx_mm(ps, wb, off, kc_cols, m_off, mcols, c0, n, pres, gi_reads):
            for k in range(KC):
                mm(ps, wb[:, off + k * kc_cols + m_off:off + k * kc_cols + m_off + mcols],
                   HX[:, k * HXW + c0:k * HXW + c0 + n], k == 0, k == KC - 1, gi_reads, pres)

        def b1_phase(l):
            wi = w_in_d[l]
            for b, (c0w, ncols) in enumerate(((4608, 256), (4864, 256), (5120, 64))):
                def comp(wb, b=b, ncols=ncols):
                    for m in range((ncols + 127) // 128):
                        rows = min(128, ncols - m * 128)
                        ch = 2 * b + m
                        for gi, (t0, t1, seg) in enumerate(TGS):
                            n = t1 - t0
                            pt, pr = psm()
                            hx_mm(pt[0:rows, 0:n], wb, 0, ncols, m * 128, rows, hxc(t0), n, pr, [HX_R[gi]])
                            t_, tr = tf()
                            act(t_[0:rows, 0:n], pt[0:rows, 0:n], AF.Copy, [pr], [tr])
                            if seg == 0:
                                dst, dr = ex2i_d[gi][ch * 128:ch * 128 + rows, 0:512], R["EX2I"]
                            else:
                                dst, dr = zkvc_d[ch * 128:ch * 128 + rows, 0:256], R["ZKVC"]
                            DEFER.append(lambda t_=t_, tr=tr, dst=dst, dr=dr, rows=rows, n=n:
                                         P.dma("pool", [(dst, t_[0:rows, 0:n])], reads=[tr], writes=[], owner=tr))
                            EXW.append(tr)
                JOBSP.append(Job([(wview(wi, 0, KC, c0w, ncols), 0, KC, ncols)], comp))

            def coll():
                deps = []
                for tr in EXW:
                    deps.append(((tr.dsem, tr.dcnt), True))
                P._need("pool", deps)
                EXW.clear()
                for i in range(2):
                    P.coll(lambda e, i=i: e.collective_compute("AllGather", ALU.bypass, replica_groups=RG,
                                                               ins=[ex2i_d[i]], outs=[ex2o_d[i]]),
                           reads=[R["EX2I"]], writes=[R["EX2O"], R["ZKVC"]], owner=R["EX2O"])
            simple(coll)

        EXW = []

        def pool_chunk(Z, zr, n, wi_, seg, dst_off):
            w = POOL_W[wi_]
            ZB, ZC = FP[:, 1040:2080], FP[:, 2080:3120]
            tt(ZB[:, 1:n + 16], Z[:, 0:n + 15], Z[:, 1:n + 16], ALU.add, [zr], [R["ZB"]])
            S, Sr = ZB, R["ZB"]
            if w >= 4:
                tt(ZC[:, 2:n + 15], ZB[:, 1:n + 14], ZB[:, 3:n + 16], ALU.add, [R["ZB"]], [R["ZC"]])
                S, Sr = ZC, R["ZC"]
            if w >= 8:
                tt(ZB[:, 4:n + 13], ZC[:, 2:n + 11], ZC[:, 6:n + 15], ALU.add, [R["ZC"]], [R["ZB"]])
                S, Sr = ZB, R["ZB"]
            if w >= 16:
                tt(ZC[:, 8:n + 8], ZB[:, 4:n + 4], ZB[:, 12:n + 12], ALU.add, [R["ZB"]], [R["ZC"]])
                S, Sr = ZC, R["ZC"]
            stt(U1[:, dst_off:dst_off + n], S[:, 8:n + 8], 1.0 / w, Z[:, 8:n + 8], ALU.mult, ALU.subtract,
                [Sr, zr], [R["POOLED"]])
            for side in range(2):
                a = 0 if side == 0 else n - 8
                rc = RCBT[:, (seg * 4 + wi_) * 16 + side * 8:(seg * 4 + wi_) * 16 + side * 8 + 8]
                t_, tr = tf()
                tt(t_[:, 0:8], S[:, 8 + a:16 + a], rc, ALU.mult, [Sr, R["RCBT"]], [tr])
                tt(U1[:, dst_off + a:dst_off + a + 8], t_[:, 0:8], Z[:, 8 + a:16 + a], ALU.subtract, [tr, zr], [R["POOLED"]])

        def pool_phase(l, tgs):
            wi = w_in_d[l]
            ZA, ZD = FP[:, 0:1040], FP[:, 3120:4160]
            for b in range(4):
                def comp(wb, b=b):
                    for m in range(2):
                        j = 2 * b + m
                        pts = []
                        for gi in (0, 1):
                            t0, t1, seg = TGS[gi]
                            pt, pr = psm()
                            hx_mm(pt[:, 0:512], wb, 0, 256, m * 128, 128, hxc(t0), 512, pr, [HX_R[gi]])
                            pts.append((pt, pr))
                        ph, phr = psa()
                        for side, c0 in enumerate((0, 1032)):
                            hx_mm(ph[:, side * 8:side * 8 + 8], wb, 0, 256, m * 128, 128, c0, 8, phr, [R["HXH"]])
                        act(ZA[:, 8:520], pts[0][0][:, 0:512], AF.Copy, [pts[0][1]], [R["ZA"]])
                        cp(ZA[:, 520:1032], pts[1][0][:, 0:512], [pts[1][1]], [R["ZA"]])
                        act(ZA[:, 0:8], ph[:, 0:8], AF.Copy, [phr], [R["ZA"]])
                        act(ZA[:, 1032:1040], ph[:, 8:16], AF.Copy, [phr], [R["ZA"]])
                        pool_chunk(ZA, R["ZA"], 1024, j // 2, 0, j * T)
                        if 2 in tgs:
                            pt, pr = psm()
                            hx_mm(pt[:, 0:256], wb, 0, 256, m * 128, 128, hxc(1024), 256, pr, [HX_R[2]])
                            memset(ZD[:, 0:8], 0.0, [R["ZD"]], eng="dve")
                            memset(ZD[:, 264:272], 0.0, [R["ZD"]], eng="dve")
                            act(ZD[:, 8:264], pt[:, 0:256], AF.Copy, [pr], [R["ZD"]])
                            pool_chunk(ZD, R["ZD"], 256, j // 2, 1, j * T + 1024)
                JOBSP.append(Job([(wview(wi, 0, KC, b * 256, 256), 0, KC, 256)], comp))
                if b % 1 == 0:
                    g = b
                    def mix(wb, g=g):
                        for mo in range(2):
                            for gi, (t0, t1, seg) in enumerate(TGS):
                                if gi not in tgs:
                                    continue
                                n = t1 - t0
                                pt, pr = psm()
                                for ki in range(2):
                                    mm(pt[:, 0:n], wb[:, ki * 256 + mo * 128:ki * 256 + mo * 128 + 128],
                                       U1[:, (2 * g + ki) * T + t0:(2 * g + ki) * T + t1], ki == 0, ki == 1, [R["POOLED"]], pr)
                                o_, orr = tb()
                                act(o_[:, 0:n], pt[:, 0:n], AF.Identity, [pr, R["VEC"]], [orr], scale=vcol(l, 128 + 2 * g + mo))
                                ch = 2 * g + mo
                                DEFER.append(lambda o_=o_, orr=orr, ch=ch, t0=t0, t1=t1, gi=gi, n=n:
                                             P.dma("pool", [(abo_d[ch * 128:(ch + 1) * 128, t0:t1], o_[:, 0:n])],
                                                   reads=[orr], writes=[ABO_R[ch][gi]], owner=orr))
                    JOBSP.append(Job([(wview(pool_w_d[l], g * 256, 2, 0, 256), 0, 2, 256)], mix))

        def conv_phase(l, tgs):
            wi = w_in_d[l]
            UU, CVL, UC, CVC = FP[:, 0:1040], FP[:, 3120:4160], FP[:, 1040:1300], FP[:, 2080:2340]

            def conv3(dst, src, n, j, sr, dr):
                ts(dst[:, 0:n], src[:, 1:n + 1], vcol(l, 136 + 8 + j), None, ALU.mult, None, [sr, R["VEC"]], [dr])
                stt(dst[:, 0:n], src[:, 0:n], vcol(l, 136 + j), dst[:, 0:n], ALU.mult, ALU.add, [sr, R["VEC"], dr], [dr])
                stt(dst[:, 0:n], src[:, 2:n + 2], vcol(l, 136 + 16 + j), dst[:, 0:n], ALU.mult, ALU.add, [sr, R["VEC"], dr], [dr])

            for j in range(8):
                def comp1(wb, j=j):
                    ph, phr = psa()
                    for q in range(2):
                        for side, c0 in enumerate((7, 1032)):
                            hx_mm(ph[:, 2 * q + side:2 * q + side + 1], wb, q * 2048, 128, 0, 128, c0, 1, phr, [R["HXH"]])
                    act(H4[:, 0:4], ph[:, 0:4], AF.Copy, [phr], [R["H4"]])
                    tt(UU[:, 0:1], H4[:, 0:1], H4[:, 2:3], ALU.mult, [R["H4"]], [R["ZA"]])
                    tt(UU[:, 1025:1026], H4[:, 1:2], H4[:, 3:4], ALU.mult, [R["H4"]], [R["ZA"]])
                    for gi in (0, 1):
                        t0, t1, seg = TGS[gi]
                        pa, par_ = psm()
                        hx_mm(pa[:, 0:512], wb, 0, 128, 0, 128, hxc(t0), 512, par_, [HX_R[gi]])
                        pb, pbr = psm()
                        hx_mm(pb[:, 0:512], wb, 2048, 128, 0, 128, hxc(t0), 512, pbr, [HX_R[gi]])
                        t_, tr = tf()
                        act(t_[:, 0:512], pa[:, 0:512], AF.Copy, [par_], [tr])
                        tt(UU[:, 1 + t0:1 + t1], t_[:, 0:512], pb[:, 0:512], ALU.mult, [tr, pbr], [R["ZA"]])
                    conv3(CVL, UU, 1024, j, R["ZA"], R["ZD"])
                    if 2 in tgs:
                        pa, par_ = psm()
                        hx_mm(pa[:, 0:256], wb, 0, 128, 0, 128, hxc(1024), 256, par_, [HX_R[2]])
                        pb, pbr = psm()
                        hx_mm(pb[:, 0:256], wb, 2048, 128, 0, 128, hxc(1024), 256, pbr, [HX_R[2]])
                        t_, tr = tf()
                        act(t_[:, 0:256], pa[:, 0:256], AF.Copy, [par_], [tr])
                        memset(UC[:, 0:1], 0.0, [R["ZB"]], eng="dve")
                        memset(UC[:, 257:258], 0.0, [R["ZB"]], eng="dve")
                        tt(UC[:, 1:257], t_[:, 0:256], pb[:, 0:256], ALU.mult, [tr, pbr], [R["ZB"]])
                        conv3(CVC, UC, 256, j, R["ZB"], R["ZC"])
                JOBSP.append(Job([(wview(wi, 0, KC, 2048 + j * 128, 128), 0, KC, 128),
                                 (wview(wi, 0, KC, 3072 + j * 128, 128), 2048, KC, 128)], comp1))

                def comp2(wb, j=j):
                    for gi, (t0, t1, seg) in enumerate(TGS):
                        if gi not in tgs:
                            continue
                        n = t1 - t0
                        pt, pr = psm()
                        hx_mm(pt[:, 0:n], wb, 0, 128, 0, 128, hxc(t0), n, pr, [HX_R[gi]])
                        o_, orr = tb()
                        if seg == 0:
                            tt(o_[:, 0:n], pt[:, 0:n], CVL[:, t0:t1], ALU.mult, [pr, R["ZD"]], [orr])
                        else:
                            tt(o_[:, 0:n], pt[:, 0:n], CVC[:, 0:256], ALU.mult, [pr, R["ZC"]], [orr])
                        ch = 8 + j
                        DEFER.append(lambda o_=o_, orr=orr, ch=ch, t0=t0, t1=t1, gi=gi, n=n:
                                     P.dma("pool", [(abo_d[ch * 128:(ch + 1) * 128, t0:t1], o_[:, 0:n])],
                                           reads=[orr], writes=[ABO_R[ch][gi]], owner=orr))
                JOBSP.append(Job([(wview(wi, 0, KC, 1024 + j * 128, 128), 0, KC, 128)], comp2))

        def gates_phase(l, tgs):
            wi = w_in_d[l]
            for b in range(24):
                def comp(wb, b=b):
                    for m in range(2):
                        ch = 2 * b + m
                        for gi, (t0, t1, seg) in enumerate(TGS):
                            if gi not in tgs:
                                continue
                            n = t1 - t0
                            pt, pr = psm()
                            hx_mm(pt[:, 0:n], wb, 0, 256, m * 128, 128, hxc(t0), n, pr, [HX_R[gi]])
                            o_, orr = tb()
                            act(o_[:, 0:n], pt[:, 0:n], AF.Sigmoid, [pr], [orr])
                            DEFER.append(lambda o_=o_, orr=orr, ch=ch, t0=t0, t1=t1, gi=gi, n=n:
                                         P.dma("pool", [(gsc_d[ch * 128:(ch + 1) * 128, t0:t1], o_[:, 0:n])],
                                               reads=[orr], writes=[GSC_R[ch][gi]], owner=orr))
                JOBSP.append(Job([(wview(wi, 0, KC, 5184 + b * 256, 256), 0, KC, 256)], comp))

        def q_phase(l, tgs):
            wi = w_in_d[l]
            for gi, (t0, t1, seg) in enumerate(TGS):
                if gi not in tgs:
                    continue
                for b in range(2):
                    def comp(wb, b=b, gi=gi, t0=t0, t1=t1):
                        n = t1 - t0
                        for m in range(2):
                            c = 2 * b + m
                            pt, pr = psm()
                            hx_mm(pt[:, 0:n], wb, 0, 256, m * 128, 128, hxc(t0), n, pr, [HX_R[gi]])
                            act(FP[:, c * 512:c * 512 + n], pt[:, 0:n], AF.Copy, [pr], [R["ZA"], R["ZB"]])
                        if b == 1:
                            pt, pr = psm()
                            for c in range(4):
                                sq, sr = tf()
                                act(sq[:, 0:n], FP[:, c * 512:c * 512 + n], AF.Square, [R["ZA"], R["ZB"]], [sr])
                                mm(pt[:, 0:n], ONESF[:], sq[:, 0:n], c == 0, c == 3, [sr, R["ONES"]], pr)
                            rt, rr = rstd_from(pt[:, 0:n], n, 512, pr)
                            for c in range(4):
                                tm, tr = tf()
                                tt(tm[:, 0:n], FP[:, c * 512:c * 512 + n], rt[:, 0:n], ALU.mult, [R["ZA"], R["ZB"], rr], [tr])
                                act(U1[:, c * T + t0:c * T + t1], tm[:, 0:n], AF.Identity, [tr, R["VEC"]], [R["CQ"]],
                                    scale=vcol(l, 160 + c))
                    JOBSP.append(Job([(wview(wi, 0, KC, 4096 + b * 256, 256), 0, KC, 256)], comp))

        CKV0 = 5120
        KRB0 = 14336
        SQKR0 = 16640

        def kv_prep(l):
            def go():
                for kg in range(5):
                    n = 512 if kg < 4 else 256
                    k0 = kg * 512
                    if kg < 4:
                        r = kg // 2
                        src = ex2o_d[kg % 2][r * 576:(r + 1) * 576, :]
                        sres = R["EX2O"]
                    else:
                        src = zkvc_d[:, :]
                        sres = R["ZKVC"]
                    pt, pr = psm()
                    for c in range(4):
                        zk, zr = tf()
                        P.dma("sp", [(zk[:, 0:n], src[c * 128:(c + 1) * 128, :])], reads=[sres], writes=[zr], owner=zr)
                        sq, sr = tf()
                        act(sq[:, 0:n], zk[:, 0:n], AF.Square, [zr], [sr])
                        mm(pt[:, 0:n], ONESF[:], sq[:, 0:n], c == 0, c == 3, [sr, R["ONES"]], pr)
                    rt, rr = rstd_from(pt[:, 0:n], n, 512, pr)
                    for c in range(4):
                        zk, zr = tf()
                        P.dma("sp", [(zk[:, 0:n], src[c * 128:(c + 1) * 128, :])], reads=[sres], writes=[zr], owner=zr)
                        tm, tr = tf()
                        tt(tm[:, 0:n], zk[:, 0:n], rt[:, 0:n], ALU.mult, [zr, rr], [tr])
                        act(U1[:, CKV0 + c * NK + k0:CKV0 + c * NK + k0 + n], tm[:, 0:n], AF.Identity, [tr, R["VEC"]],
                            [R["CKV"]], scale=vcol(l, 164 + c))
                    kr, krr = tf()
                    P.dma("sp", [(kr[0:64, 0:n], src[512:576, :])], reads=[sres], writes=[krr], owner=krr)
                    act(HX[0:64, SQKR0 + k0:SQKR0 + k0 + n], kr[0:64, 0:n], AF.Square, [krr], [R["KRB"]])
                    co, cor = tf()
                    P.dma("sp", [(co[0:64, 0:n], ropek_d[0][:, k0:k0 + n])], writes=[cor], owner=cor)
                    t1_, t1r = tf()
                    stt(t1_[0:64, 0:n], kr[0:64, 0:n], vcol(l, 172, 64), co[0:64, 0:n], ALU.mult, ALU.mult,
                        [krr, cor, R["VEC"]], [t1r])
                    ks, ksr = tf()
                    P.dma("sp", [(ks[0:16, 0:n], src[528:544, :]), (ks[16:32, 0:n], src[512:528, :]),
                                 (ks[32:48, 0:n], src[560:576, :]), (ks[48:64, 0:n], src[544:560, :])],
                          reads=[sres], writes=[ksr], owner=ksr)
                    si, sir = tf()
                    P.dma("sp", [(si[0:64, 0:n], ropek_d[1][:, k0:k0 + n])], writes=[sir], owner=sir)
                    t2_, t2r = tf()
                    stt(t2_[0:64, 0:n], ks[0:64, 0:n], vcol(l, 173, 64), si[0:64, 0:n], ALU.mult, ALU.mult,
                        [ksr, sir, R["VEC"]], [t2r])
                    tt(HX[0:64, KRB0 + k0:KRB0 + k0 + n], t1_[0:64, 0:n], t2_[0:64, 0:n], ALU.add, [t1r, t2r], [R["KRB"]])
            simple(go)

        def attn_phase(l, tgs):
            def tiles(h):
                hs = h % 2
                base = hs * 7168
                return dict(KTN=HX[:, base:base + NK], KTR=HX[0:64, KRB0:KRB0 + NK], RK=RKt[hs],
                            VV=HX[:, base + 2304:base + 2304 + NK], QTN=HX[:, base + 4608:base + 4608 + T],
                            QTR=HX[0:64, base + 5888:base + 5888 + T], hr=HS_R[hs])

            def xcast(wsb, wbb):
                src = wsb[:, 1024:1792].rearrange("p (k c) -> p k c", k=4)
                dst = wbb[:, 1792:2048].rearrange("p (k c) -> p k c", k=4)
                for (d0, s0) in ((0, 144), (16, 128), (32, 176), (48, 160)):
                    yield dst[:, :, d0:d0 + 16], src[:, :, s0:s0 + 16]

            def prep_steps(h, wb):
                tl = tiles(h)
                KTN, KTR, VV, QTN, QTR, hr = tl["KTN"], tl["KTR"], tl["VV"], tl["QTN"], tl["QTR"], tl["hr"]
                RK = tl["RK"]
                rkp, rkr = PSt[4], PS_R[4]
                for kg in range(5):
                    n = 512 if kg < 4 else 256
                    k0 = kg * 512
                    pt, pr = psp()
                    for k in range(4):
                        mm(pt[:, 0:n], wb[:, k * 256:k * 256 + 128], U1[:, CKV0 + k * NK + k0:CKV0 + k * NK + k0 + n],
                           k == 0, k == 3, [R["CKV"]], pr)
                    yield
                    sq, sr = sqb()
                    act(sq[:, 0:n], pt[:, 0:n], AF.Square, [pr], [sr])
                    act(KTN[:, k0:k0 + n], pt[:, 0:n], AF.Identity, [pr, R["VEC"]], [hr["K"]], scale=vcol(l, 171))
                    yield
                    for c in range(n // 128):
                        kc = kg * 4 + c
                        mm(rkp[:, kc:kc + 1], sq[:, c * 128:(c + 1) * 128], ONESB[:, 0:1], True, False, [sr, R["ONES"]], rkr)
                        mm(rkp[:, kc:kc + 1], HX[0:64, SQKR0 + kc * 128:SQKR0 + (kc + 1) * 128], ONESB[0:64, 0:1], False, True,
                           [R["KRB"], R["ONES"]], rkr)
                    yield
                act(RK[:, 0:18], rkp[:, 0:18], AF.Ln, [rkr, R["ONES"]], [hr["K"]], bias=EPST[:, 0:1], scale=1.0 / 192)
                act(RK[:, 0:18], RK[:, 0:18], AF.Exp, [hr["K"]], [hr["K"]], scale=-0.5)
                ts(RK[:, 0:18], RK[:, 0:18], ATTN_SCALE, None, ALU.mult, None, [hr["K"]], [hr["K"]])
                yield
                for kc4 in range(0, 18, 4):
                    cnt_ = min(4, 18 - kc4)
                    pt, pr = psp()
                    for i in range(cnt_):
                        kc = kc4 + i
                        for k in range(4):
                            mm(pt[:, i * 128:(i + 1) * 128], U1[:, CKV0 + k * NK + kc * 128:CKV0 + k * NK + kc * 128 + 128],
                               wb[:, k * 256 + 128:k * 256 + 256], k == 0, k == 3, [R["CKV"]], pr)
                    yield
                    act(VV[:, kc4 * 128:(kc4 + cnt_) * 128], pt[:, 0:cnt_ * 128], AF.Copy, [pr], [hr["V"]])
                    yield
                for gi, (t0, t1, seg) in enumerate(TGS):
                    if gi not in tgs:
                        continue
                    n = t1 - t0
                    pa, par_ = psp()
                    pb, pbr = psp()
                    for k in range(4):
                        mm(pa[:, 0:n], wb[:, 1024 + k * 192:1024 + k * 192 + 128], U1[:, k * T + t0:k * T + t1],
                           k == 0, k == 3, [R["CQ"]], par_)
                    for k in range(4):
                        mm(pb[0:64, 0:n], wb[:, 1024 + k * 192 + 128:1024 + k * 192 + 192], U1[:, k * T + t0:k * T + t1],
                           k == 0, k == 3, [R["CQ"]], pbr)
                    yield
                    sqa, sar = sqb()
                    act(sqa[:, 0:n], pa[:, 0:n], AF.Square, [par_], [sar])
                    sqb_, sbr = sqb()
                    act(sqb_[0:64, 0:n], pb[0:64, 0:n], AF.Square, [pbr], [sbr])
                    yield
                    pd, pdr = psp()
                    mm(pd[:, 0:n], ONESB[:], sqa[:, 0:n], True, False, [sar, R["ONES"]], pdr)
                    mm(pd[:, 0:n], ONESB[0:64, :], sqb_[0:64, 0:n], False, True, [sbr, R["ONES"]], pdr)
                    yield
                    rt, rr = rstd_from(pd[:, 0:n], n, 192, pdr)
                    yield
                    stt(QTN[:, t0:t1], pa[:, 0:n], vcol(l, 168), rt[:, 0:n], ALU.mult, ALU.mult, [par_, rr, R["VEC"]], [hr["Q"]])
                    bg, bgr = tf()
                    act(bg[0:64, 0:n], pb[0:64, 0:n], AF.Identity, [pbr, R["VEC"]], [bgr], scale=vcol(l, 169, 64))
                    tt(bg[0:64, 0:n], bg[0:64, 0:n], ROPEQ[:, t0:t1], ALU.mult, [bgr, R["ROPEQ"]], [bgr])
                    yield
                    pc, pcr = psp()
                    for k in range(4):
                        mm(pc[0:64, 0:n], wb[:, 1792 + k * 64:1792 + k * 64 + 64], U1[:, k * T + t0:k * T + t1],
                           k == 0, k == 3, [R["CQ"]], pcr)
                    yield
                    cg, cgr = tf()
                    act(cg[0:64, 0:n], pc[0:64, 0:n], AF.Identity, [pcr, R["VEC"]], [cgr], scale=vcol(l, 170, 64))
                    tt(cg[0:64, 0:n], cg[0:64, 0:n], ROPEQ[:, T + t0:T + t1], ALU.mult, [cgr, R["ROPEQ"]], [cgr])
                    tt(bg[0:64, 0:n], bg[0:64, 0:n], cg[0:64, 0:n], ALU.add, [bgr, cgr], [bgr])
                    tt(QTR[:, t0:t1], bg[0:64, 0:n], rt[0:64, 0:n], ALU.mult, [bgr, rr], [hr["Q"]])
                    yield

            def spv(h, gen):
                tl = tiles(h)
                KTN, KTR, VV, QTN, QTR, hr = tl["KTN"], tl["KTR"], tl["VV"], tl["QTN"], tl["QTR"], tl["hr"]
                RK = tl["RK"]
                it = 0
                for gi, (t0, t1, seg) in enumerate(TGS):
                    if gi not in tgs:
                        continue
                    n = t1 - t0
                    kcs = list(range(18)) if seg == 0 else [16, 17]
                    po, por = PSt[2], PS_R[2]
                    pss, psr = PSt[3], PS_R[3]

                    def smm(i):
                        kc = kcs[i]
                        sp_, spr = PSt[i % 2], PS_R[i % 2]
                        mm(sp_[:, 0:n], KTN[:, kc * 128:(kc + 1) * 128], QTN[:, t0:t1], True, False, [hr["K"], hr["Q"]], spr)
                        mm(sp_[:, 0:n], KTR[:, kc * 128:(kc + 1) * 128], QTR[:, t0:t1], False, True, [R["KRB"], hr["Q"]], spr)
                    smm(0)
                    for i in range(len(kcs)):
                        if i + 1 < len(kcs):
                            smm(i + 1)
                        kc = kcs[i]
                        sp_, spr = PSt[i % 2], PS_R[i % 2]
                        pt_, ptr = tb()
                        act(pt_[:, 0:n], sp_[:, 0:n], AF.Exp, [spr, hr["K"]], [ptr], scale=RK[:, kc:kc + 1])
                        mm(po[:, 0:n], VV[:, kc * 128:(kc + 1) * 128], pt_[:, 0:n], i == 0, i == len(kcs) - 1, [hr["V"], ptr], por)
                        mm(pss[:, 0:n], ONESB[:], pt_[:, 0:n], i == 0, i == len(kcs) - 1, [R["ONES"], ptr], psr)
                        it += 1
                        next(gen, None)
                    rc, rcr = tf()
                    act(rc[:, 0:n], pss[:, 0:n], AF.Ln, [psr], [rcr])
                    act(rc[:, 0:n], rc[:, 0:n], AF.Exp, [rcr], [rcr], scale=-1.0)
                    o_, orr = tb()
                    tt(o_[:, 0:n], po[:, 0:n], rc[:, 0:n], ALU.mult, [por, rcr], [orr])
                    ch = 16 + h
                    P.dma("pool", [(abo_d[ch * 128:(ch + 1) * 128, t0:t1], o_[:, 0:n])],
                          reads=[orr], writes=[ABO_R[ch][gi]], owner=orr)
                for _ in gen:
                    pass

            def hspecs(h):
                return [(wview(w_ukv_d[l], 0, 4, h * 256, 256), 0, 4, 256),
                        (wview(w_uq_d[l], 0, 4, h * 192, 192), 1024, 4, 192)]

            def first(wb):
                for _ in prep_steps(0, wb):
                    pass
            JOBSP.append(Job(hspecs(0), first, extra_cast=xcast))
            for h in range(16):
                if h < 15:
                    JOBSP.append(Job(hspecs(h + 1), lambda wb, h=h: spv(h, prep_steps(h + 1, wb)), extra_cast=xcast))
                else:
                    JOBSP.append(Job(None, lambda wb, h=h: spv(h, iter(()))))

        def merge_phase(l, tgs):
            for gi, (t0, t1, seg) in enumerate(TGS):
                if gi not in tgs:
                    continue
                n = t1 - t0

                def load(gi=gi, t0=t0, t1=t1, n=n):
                    av = abo_d.rearrange("(c p) t -> p c t", p=128)
                    for q in range(4):
                        P.dma("sp", [(U1[:, q * 8 * 512:(q + 1) * 8 * 512].rearrange("p (c t) -> p c t", c=8)[:, :, 0:n],
                                      av[:, q * 8:(q + 1) * 8, t0:t1])],
                              reads=[ABO_R[c][gi] for c in range(q * 8, q * 8 + 8)], writes=[R["ABO"]], owner=R["ABO"])
                simple(load)
                for j in range(16):
                    first_tg = gi == tgs[0]

                    def comp(wb, j=j, gi=gi, t0=t0, t1=t1, n=n, seg=seg, first_tg=first_tg):
                        if first_tg:
                            P.dma("pool", [(wcache_d[j * 128:(j + 1) * 128, :], wb[:, 0:4096])], reads=[CUR["wb"]],
                                  writes=[WC_R[j]], owner=CUR["wb"])
                        g_, gr = gt()
                        gv = gsc_d.rearrange("(b j p) t -> p b j t", p=128, b=3)
                        P.dma("sp", [(g_[:].rearrange("p (b t) -> p b t", b=3)[:, :, 0:n], gv[:, :, j, t0:t1])],
                              reads=[GSC_R[b * 16 + j][gi] for b in range(3)], writes=[gr], owner=gr)
                        pa, par_ = psm()
                        for k in range(8):
                            mm(pa[:, 0:n], wb[:, k * 128:(k + 1) * 128], U1[:, k * 512:k * 512 + n], k == 0, k == 7, [R["ABO"]], par_)
                        pb, pbr = psm()
                        for k in range(8):
                            mm(pb[:, 0:n], wb[:, 1024 + k * 128:1024 + (k + 1) * 128], U1[:, (8 + k) * 512:(8 + k) * 512 + n],
                               k == 0, k == 7, [R["ABO"]], pbr)
                        pc, pcr = psm()
                        for k in range(16):
                            mm(pc[:, 0:n], wb[:, 2048 + k * 128:2048 + (k + 1) * 128], U1[:, (16 + k) * 512:(16 + k) * 512 + n],
                               k == 0, k == 15, [R["ABO"]], pcr)
                        m1, m1r = tf()
                        tt(m1[:, 0:n], pa[:, 0:n], g_[:, 0:n], ALU.mult, [par_, gr], [m1r])
                        m2, m2r = tf()
                        tt(m2[:, 0:n], pb[:, 0:n], g_[:, 512:512 + n], ALU.mult, [pbr, gr], [m2r])
                        tt(m1[:, 0:n], m1[:, 0:n], m2[:, 0:n], ALU.add, [m1r, m2r], [m1r])
                        m3, m3r = tf()
                        tt(m3[:, 0:n], pc[:, 0:n], g_[:, 1024:1024 + n], ALU.mult, [pcr, gr], [m3r])
                        c0 = hxc(t0)
                        tt(HX[:, j * HXW + c0:j * HXW + c0 + n], m1[:, 0:n], m3[:, 0:n], ALU.add, [m1r, m3r], [HX_R[gi]])
                    if first_tg:
                        JOBSP.append(Job([(wview(w_ba_d[l], 0, 8, j * 128, 128), 0, 8, 128),
                                          (wview(w_bb_d[l], 0, 8, j * 128, 128), 1024, 8, 128),
                                          (wview(w_bc_d[l], 0, 16, j * 128, 128), 2048, 16, 128)], comp))
                    else:
                        JOBSP.append(Job([], comp, bf16_src=(wcache_d[j * 128:(j + 1) * 128, :], 4096, WC_R[j])))
            for b in range(8):
                def comp(wb, b=b):
                    for m in range(2):
                        i = 2 * b + m
                        for gi, (t0, t1, seg) in enumerate(TGS):
                            if gi not in tgs:
                                continue
                            n = t1 - t0
                            pt, pr = psm()
                            hx_mm(pt[:, 0:n], wb, 0, 256, m * 128, 128, hxc(t0), n, pr, [HX_R[gi]])
                            xk, xr = tf()
                            P.dma("sp", [(xk[:, 0:n], xs_view(i, t0, t1))], reads=[XS_R[i][gi]], writes=[xr], owner=xr)
                            xn, xnr = tf()
                            c = l * 192 + 2 * 32 + seg * 16 + i
                            stt(xn[:, 0:n], pt[:, 0:n], PAR[:, c:c + 1], xk[:, 0:n], ALU.mult, ALU.add, [pr, xr, PAR_R[l]], [xnr])
                            DEFER.append(lambda xn=xn, xnr=xnr, i=i, t0=t0, t1=t1, gi=gi, n=n:
                                         P.dma("pool", [(xs_view(i, t0, t1), xn[:, 0:n])], reads=[xnr], writes=[XS_R[i][gi]], owner=xnr))
                JOBSP.append(Job([(wview(w_out_d[l], 0, KC, b * 256, 256), 0, KC, 256)], comp))

        def ffn_phase(l, tgs, last):
            NQ, QC = 4, 11
            fc = 174
            for q in range(NQ):
                for jj in range(QC):
                    j = q * QC + jj

                    def comp(wb, j=j, jj=jj):
                        UE, CVf, SL = FP[:, 0:1026], FP[:, 1040:2064], FP[:, 2080:3104]
                        UEc, CVc, SLc = FP[:, 3120:3378], FP[:, 3400:3656], FP[:, 3700:3956]
                        pus, pvs = [], []
                        for gi in (0, 1):
                            t0, t1, seg = TGS[gi]
                            pu, pur = psm()
                            hx_mm(pu[:, 0:512], wb, 0, 128, 0, 128, hxc(t0), 512, pur, [HX_R[gi]])
                            pus.append((pu, pur))
                        ph, phr = psa()
                        for k in range(KC):
                            mm(ph[:, 0:2], wb[:, k * 128:k * 128 + 128], HX[:, k * HXW + 7:k * HXW + 1033:1025],
                               k == 0, k == KC - 1, [R["HXH"]], phr)
                        for gi in (0, 1):
                            t0, t1, seg = TGS[gi]
                            pv, pvr = psm()
                            hx_mm(pv[:, 0:512], wb, 2048, 128, 0, 128, hxc(t0), 512, pvr, [HX_R[gi]])
                            pvs.append((pv, pvr))
                        act(UE[:, 1:513], pus[0][0][:, 0:512], AF.Copy, [pus[0][1]], [R["ZA"]])
                        act(UE[:, 513:1025], pus[1][0][:, 0:512], AF.Copy, [pus[1][1]], [R["ZA"]])
                        act(UE[:, 0:1026:1025], ph[:, 0:2], AF.Copy, [phr], [R["ZA"]])
                        ts(CVf[:, 0:1024], UE[:, 1:1025], vcol(l, fc + 44 + j), None, ALU.mult, None, [R["ZA"], R["VEC"]], [R["ZB"]])
                        stt(CVf[:, 0:1024], UE[:, 0:1024], vcol(l, fc + j), CVf[:, 0:1024], ALU.mult, ALU.add,
                            [R["ZA"], R["VEC"], R["ZB"]], [R["ZB"]])
                        stt(CVf[:, 0:1024], UE[:, 2:1026], vcol(l, fc + 88 + j), CVf[:, 0:1024], ALU.mult, ALU.add,
                            [R["ZA"], R["VEC"], R["ZB"]], [R["ZB"]])
                        act(SL[:, 0:1024], CVf[:, 0:1024], AF.Silu, [R["ZB"]], [R["ZC"]])
                        for gi in (0, 1):
                            t0, t1, seg = TGS[gi]
                            tt(U1[:, jj * T + t0:jj * T + t1], SL[:, t0:t1], pvs[gi][0][:, 0:512], ALU.mult,
                               [R["ZC"], pvs[gi][1]], [R["FF"]])
                        if 2 in tgs:
                            t0, t1, seg = TGS[2]
                            pu, pur = psm()
                            hx_mm(pu[:, 0:256], wb, 0, 128, 0, 128, hxc(t0), 256, pur, [HX_R[2]])
                            pv, pvr = psm()
                            hx_mm(pv[:, 0:256], wb, 2048, 128, 0, 128, hxc(t0), 256, pvr, [HX_R[2]])
                            memset(UEc[:, 0:258:257], 0.0, [R["ZD"]], eng="dve")
                            act(UEc[:, 1:257], pu[:, 0:256], AF.Copy, [pur], [R["ZD"]])
                            ts(CVc[:, 0:256], UEc[:, 1:257], vcol(l, fc + 44 + j), None, ALU.mult, None, [R["ZD"], R["VEC"]], [R["ZD"]])
                            stt(CVc[:, 0:256], UEc[:, 0:256], vcol(l, fc + j), CVc[:, 0:256], ALU.mult, ALU.add,
                                [R["ZD"], R["VEC"]], [R["ZD"]])
                            stt(CVc[:, 0:256], UEc[:, 2:258], vcol(l, fc + 88 + j), CVc[:, 0:256], ALU.mult, ALU.add,
                                [R["ZD"], R["VEC"]], [R["ZD"]])
                            act(SLc[:, 0:256], CVc[:, 0:256], AF.Silu, [R["ZD"]], [R["ZD"]])
                            tt(U1[:, jj * T + t0:jj * T + t1], SLc[:, 0:256], pv[:, 0:256], ALU.mult, [R["ZD"], pvr], [R["FF"]])
                    JOBSP.append(Job([(wview(w_up_d[l], 0, KC, j * 128, 128), 0, KC, 128),
                                      (wview(w_up_d[l], 0, KC, DFF + j * 128, 128), 2048, KC, 128)], comp))
                for ip in range(8):
                    def comp(wb, ip=ip, q=q):
                        for m in range(2):
                            i = 2 * ip + m
                            for gi, (t0, t1, seg) in enumerate(TGS):
                                if gi not in tgs:
                                    continue
                                n = t1 - t0
                                pt, pr = psm()
                                for k in range(QC):
                                    mm(pt[:, 0:n], wb[:, k * 256 + m * 128:k * 256 + m * 128 + 128], U1[:, k * T + t0:k * T + t1],
                                       k == 0, k == QC - 1, [R["FF"]], pr)
                                xk, xr = tf()
                                P.dma("sp", [(xk[:, 0:n], xs_view(i, t0, t1))], reads=[XS_R[i][gi]], writes=[xr], owner=xr)
                                xn, xnr = tf()
                                c = l * 192 + 5 * 32 + seg * 16 + i
                                stt(xn[:, 0:n], pt[:, 0:n], PAR[:, c:c + 1], xk[:, 0:n], ALU.mult, ALU.add, [pr, xr, PAR_R[l]], [xnr])
                                if last and q == NQ - 1:
                                    OUT_TOKS.append(P.dma("pool", [(outT_d[i * 128:(i + 1) * 128, t0:t1], xn[:, 0:n])],
                                                          reads=[xnr], writes=[], owner=xnr))
                                else:
                                    P.dma("pool", [(xs_view(i, t0, t1), xn[:, 0:n])], reads=[xnr], writes=[XS_R[i][gi]], owner=xnr)
                    JOBSP.append(Job([(wview(w_dn_d[l], q * QC * 128, QC, ip * 256, 256), 0, QC, 256)], comp))

        def spread(main, extra):
            nw = sum(1 for j in main if j.specs)
            out, k, seen = [], 0, 0
            for j in main:
                out.append(j)
                if j.specs:
                    seen += 1
                    while k < len(extra) and (k + 1) * nw <= seen * len(extra):
                        out.append(extra[k])
                        k += 1
            out.extend(extra[k:])
            return out

        def collect(fn):
            lst = []
            SINK.append(lst)
            fn()
            SINK.pop()
            return lst

        for l in range(nlayers):
            last = l == nlayers - 1
            full = (0, 1, 2)
            tg_main = (0, 1) if last else full
            HSALL = [HS_R[i][n_] for i in range(2) for n_ in ("K", "V", "Q")]

            def part_a(l=l, tg_main=tg_main, full=full):
                b1_phase(l)
                inherit([R["POOLED"]], [R["FF"], R["ABO"], R["CQ"], R["CKV"]])
                inherit([R["ZA"], R["ZB"], R["ZC"], R["ZD"]], [R["KR"], R["SSR"]])
                pool_phase(l, tg_main)
                conv_phase(l, tg_main)
                gates_phase(l, tg_main)
                inherit([R["CQ"], R["CKV"]], [R["POOLED"], R["FF"], R["ABO"]])
                q_phase(l, tg_main)
                inherit([R["KR"], R["SSR"]], [R["ZA"], R["ZB"], R["ZC"], R["ZD"]])
                inherit([R["KRB"]], HX_R + [R["HXH"]])
                kv_prep(l)
                inherit(HSALL, HX_R + [R["HXH"]])
                attn_phase(l, tg_main)
                inherit(HX_R + [R["HXH"]], HSALL + [R["KRB"]])
                inherit([R["ABO"]], [R["CQ"], R["CKV"], R["POOLED"], R["FF"]])

            def part_b(l=l, tg_main=tg_main, last=last):
                merge_phase(l, tg_main)
                exchange_edges(1)
                norm_phase(l, 1, tg_main, 1)
                inherit([R["FF"]], [R["ABO"], R["CQ"], R["CKV"], R["POOLED"]])
                inherit([R["ZA"], R["ZB"], R["ZC"], R["ZD"]], [R["KR"], R["SSR"]])
                ffn_phase(l, tg_main, last)

            if l == 0:
                mods_blocks(0, 0, 8)
                mods_gather(0, [0, 1])
                mods_final(0, 0)
            exchange_edges(8)
            norm_phase(l, 0, full, 8)
            ja = collect(part_a)
            if l == 0:
                JOBS.extend(spread(ja, collect(lambda: mods_blocks(0, 8, 24))))
                mods_gather(0, [2, 3, 4, 5])
                mods_final(0, 1)
            else:
                JOBS.extend(ja)
            jb = collect(part_b)
            if l == 0 and nlayers > 1:
                JOBS.extend(spread(jb, collect(lambda: mods_blocks(1, 0, 24))))
                mods_gather(1, [0, 1, 2, 3, 4, 5])
                mods_final(1, 0)
                mods_final(1, 1)
            else:
                JOBS.extend(jb)
            print('layer', l, 'jobs so far', len(JOBS))

        if max_jobs is not None:
            JOBS = JOBS[:max_jobs]
        Wj = [j for j in JOBS if j.specs or j.bf16_src is not None]

        def load(i):
            b = i % 2
            if Wj[i].bf16_src is not None:
                return
            pairs = []
            for (src, off, kc, ncols) in Wj[i].specs:
                pairs.append((WS[b][:, off:off + kc * ncols].rearrange("p (k c) -> p k c", k=kc), src))
            P.dma("sp", pairs, writes=[WS_R[b]], owner=WS_R[b])

        def cast(i):
            b = i % 2
            if Wj[i].bf16_src is not None:
                src, nn, sres = Wj[i].bf16_src
                P.dma("sp", [(WB[b][:, 0:nn], src)], reads=[sres], writes=[WB_R[b]], owner=WB_R[b])
                return
            tot = max(off + kc * ncols for (_, off, kc, ncols) in Wj[i].specs)
            cp(WB[b][:, 0:tot], WS[b][:, 0:tot], [WS_R[b]], [WB_R[b]], eng=CAST_ENG)
            if Wj[i].extra_cast is not None:
                for (o, s) in Wj[i].extra_cast(WS[b], WB[b]):
                    cp(o, s, [WS_R[b]], [WB_R[b]], eng=CAST_ENG)

        if len(Wj) > 0:
            load(0)
        if len(Wj) > 1:
            load(1)
        if len(Wj) > 0:
            cast(0)
        wi_ = 0
        for job in JOBS:
            if job.specs or job.bf16_src is not None:
                b = wi_ % 2
                if wi_ + 1 < len(Wj):
                    cast(wi_ + 1)
                if wi_ + 2 < len(Wj):
                    load(wi_ + 2)
                CUR["wb"] = WB_R[b]
                job.compute(WB[b])
                CUR["wb"] = None
                wi_ += 1
            else:
                job.compute(None)
        for tok in OUT_TOKS:
            P.wait_tok("pool", tok)
        if debug:
            P.drain_all("pool")
        print("streams", {k: len(v) for k, v in P.streams.items()}, "jobs", len(JOBS), "sems", P.nsem)
        P.finalize(block)
    return nc


_PERM = np.concatenate([np.arange(16, 32), np.arange(0, 16), np.arange(48, 64), np.arange(32, 48)])
_NC_CACHE = {}


def _fm(v, ncol):
    return np.ascontiguousarray(np.asarray(v, np.float32).reshape(ncol, 128).T)


def _rope_tables(row, col):
    inv = (10000.0 ** (-np.arange(0, 32, 2, dtype=np.float32) / np.float32(32))).astype(np.float32)
    ar = (row[None, :].astype(np.float32) * inv[:, None]).astype(np.float32)
    ac = (col[None, :].astype(np.float32) * inv[:, None]).astype(np.float32)
    cos = np.concatenate([np.cos(ar), np.cos(ar), np.cos(ac), np.cos(ac)], axis=0).astype(np.float32)
    ssin = np.concatenate([-np.sin(ar), np.sin(ar), -np.sin(ac), np.sin(ac)], axis=0).astype(np.float32)
    return cos, ssin


def _prepare(inp):
    g = {k: np.asarray(v) for k, v in inp.items()}
    x, c, ctx, c_ctx = g["x"], g["c"], g["ctx"], g["c_ctx"]
    vecs = np.zeros((2, 128, NV), np.float32)
    for l in range(2):
        v = vecs[l]
        v[:, 0:16] = _fm(g["norm1_g"][l], 16)
        v[:, 16:32] = _fm(g["norm2_g"][l], 16)
        v[:, 32:128] = _fm(g["b_mod"][l], 96)
        v[:, 128:136] = _fm(g["pool_scale"][l], 8)
        for t in range(3):
            v[:, 136 + t * 8:136 + t * 8 + 8] = _fm(g["conv_w"][l, t], 8)
        v[:, 160:164] = _fm(g["q_lora_g"][l], 4)
        v[:, 164:168] = _fm(g["kv_lora_g"][l], 4)
        qh, kh = g["q_head_g"][l], g["k_head_g"][l]
        v[:, 168] = qh[0:128]
        v[0:64, 169] = qh[128:192]
        v[0:64, 170] = qh[128:192][_PERM]
        v[:, 171] = kh[0:128]
        v[0:64, 172] = kh[128:192]
        v[0:64, 173] = kh[128:192][_PERM]
        for t in range(3):
            v[:, 174 + t * 44:174 + (t + 1) * 44] = _fm(g["ffn_conv"][l, t], 44)
    tl = np.arange(2048)
    ck, sk = _rope_tables(tl // 64, tl % 64)
    ropek = np.zeros((2, 64, NK), np.float32)
    ropek[0, :, :2048] = ck
    ropek[0, :, 2048:] = 1.0
    ropek[1, :, :2048] = sk
    shared = {
        "vecs": vecs, "ropek": ropek,
        "w_in": np.ascontiguousarray(g["w_in"], np.float32),
        "pool_w": np.ascontiguousarray(g["pool_w"], np.float32).reshape(2, 1024, 256),
        "w_uq": np.ascontiguousarray(g["w_uq"], np.float32).reshape(2, 512, 3072),
        "w_ukv": np.ascontiguousarray(g["w_ukv"], np.float32).reshape(2, 512, 4096),
        "w_branch_a": np.ascontiguousarray(g["w_branch_a"], np.float32),
        "w_branch_b": np.ascontiguousarray(g["w_branch_b"], np.float32),
        "w_branch_c": np.ascontiguousarray(g["w_branch_c"], np.float32),
        "w_out": np.ascontiguousarray(g["w_out"], np.float32),
        "w_ffn_up": np.ascontiguousarray(g["w_ffn_up"], np.float32),
        "w_ffn_down": np.ascontiguousarray(g["w_ffn_down"], np.float32),
    }
    in_maps = []
    for r in range(8):
        b, half = r // 2, r % 2
        s0 = half * 1024
        m = dict(shared)
        m["xT"] = np.ascontiguousarray(x[b, s0:s0 + 1024, :].T, np.float32)
        m["ctxT"] = np.ascontiguousarray(ctx[b].T, np.float32)
        cv = np.zeros((128, 80), np.float32)
        for j in range(4):
            cv[:, j * 16:(j + 1) * 16] = _fm(c[j], 16)
        cv[:, 64:80] = _fm(c_ctx, 16)
        m["cvec"] = cv
        oh = np.zeros((128, 4), np.float32)
        oh[:, b] = 1.0
        m["oh"] = oh
        m["w_mod"] = np.ascontiguousarray(np.concatenate(
            [g["w_mod"][:, :, wh * 2048 + half * 1024:wh * 2048 + (half + 1) * 1024] for wh in range(6)], axis=2), np.float32)
        rq = np.zeros((2, 64, T), np.float32)
        rq[0, :, :1024] = ck[:, s0:s0 + 1024]
        rq[0, :, 1024:] = 1.0
        rq[1, :, :1024] = sk[:, s0:s0 + 1024]
        m["ropeq"] = rq
        rcb = np.zeros((128, 128), np.float32)
        for seg, (L, starts) in enumerate(((2048, (s0, s0 + 1016)), (256, (0, 248)))):
            for wi_, w in enumerate(POOL_W):
                h = w // 2
                for side in range(2):
                    for i in range(8):
                        t = starts[side] + i
                        cntv = min(t + h, L) - max(t - h, 0)
                        rcb[:, (seg * 4 + wi_) * 16 + side * 8 + i] = 1.0 / cntv
        m["rcb"] = rcb
        mk = np.zeros((128, 2), np.float32)
        mk[:, 0] = 1.0 if half == 1 else 0.0
        mk[:, 1] = 1.0 if half == 0 else 0.0
        m["mask"] = mk
        in_maps.append(m)
    return in_maps


def kernel(**inp):
    in_maps = _prepare(inp)
    if "nc" not in _NC_CACHE:
        _NC_CACHE["nc"] = build_program(2)
    res = run_bass_kernel_spmd(_NC_CACHE["nc"], in_maps, core_ids=list(range(8)))
    out = np.zeros((4, 2048, 2048), np.float32)
    for r in range(8):
        b, half = r // 2, r % 2
        out[b, half * 1024:(half + 1) * 1024, :] = np.asarray(res.results[r]["outT"], np.float32).T
    return out
```
